# Optimizing a Trainium2 kernel written in Bass

```python
import jax, jax.numpy as jnp
from jax import lax
import numpy as np


D_MODEL = 1024
BATCH = 8
SEQ = 4096
DEPTH = 1

MIX_WIDTH = D_MODEL
ATTN_WIDTH = MIX_WIDTH // 2
ATTN_HEADS = 8
ATTN_HEAD_DIM = ATTN_WIDTH // ATTN_HEADS
DILATED_PATTERNS = ((128, 1), (512, 4), (2048, 16))
ATTN_BLOCK = 128
ATTN_PAD_UNIT = ATTN_BLOCK * 16
HGRN_WIDTH = MIX_WIDTH - ATTN_WIDTH
HGRN_EXPAND = 128
HGRN_HEADS = HGRN_WIDTH // HGRN_EXPAND
HGRN_FDIM = HGRN_EXPAND
HGRN_IDIM = HGRN_WIDTH // HGRN_HEADS
HGRN_CHUNK = 64
D_FF = 2816
CONV_WIDTH = 3
NORM_EPS = 1e-6
IN_SIZES = (ATTN_WIDTH, ATTN_WIDTH, ATTN_WIDTH, HGRN_WIDTH, HGRN_WIDTH, HGRN_WIDTH, HGRN_WIDTH)
IN_TOTAL = sum(IN_SIZES)
IN_SPLITS = [sum(IN_SIZES[:j + 1]) for j in range(len(IN_SIZES) - 1)]

kernel_name = 'hybrid_dilated_attn_hgrn2_convglu'


def rms_norm(x, g):
    xf = x.astype(jnp.float32)
    y = xf * lax.rsqrt(jnp.mean(xf * xf, axis=-1, keepdims=True) + NORM_EPS)
    return (y * g.astype(jnp.float32)).astype(x.dtype)


def dilated_branch(q, k, v, window, dilation):
    B, Sp, H, E = q.shape
    n_keys = window // dilation
    nb = Sp // (dilation * ATTN_BLOCK)
    shape = (B, nb, ATTN_BLOCK, dilation, H, E)
    qb, kb, vb = q.reshape(shape), k.reshape(shape), v.reshape(shape)

    def with_prev(t):
        prev = jnp.pad(t, ((0, 0), (1, 0), (0, 0), (0, 0), (0, 0), (0, 0)))[:, :-1]
        return jnp.concatenate([prev, t], axis=2)

    kw, vw = with_prev(kb), with_prev(vb)
    s = jnp.einsum('bnqrhe,bnkrhe->bnrhqk', qb, kw,
                   preferred_element_type=jnp.float32) * (E ** -0.5)
    qi = jnp.arange(ATTN_BLOCK)[:, None]
    kj = jnp.arange(2 * ATTN_BLOCK)[None, :]
    dist = qi + ATTN_BLOCK - kj
    blk = jnp.arange(nb)[:, None, None]
    valid = (dist >= 0) & (dist <= n_keys) & (blk * ATTN_BLOCK + kj - ATTN_BLOCK >= 0)
    s = jnp.where(valid[None, :, None, None], s, -jnp.inf)
    m = jnp.max(s, axis=-1)
    p = jnp.exp(s - m[..., None])
    l = jnp.sum(p, axis=-1)
    o = jnp.einsum('bnrhqk,bnkrhe->bnqrhe', p.astype(v.dtype), vw,
                   preferred_element_type=jnp.float32).reshape(B, Sp, H, E)
    m = m.transpose(0, 1, 4, 2, 3).reshape(B, Sp, H)
    l = l.transpose(0, 1, 4, 2, 3).reshape(B, Sp, H)
    return o, m, l


def dilated_attention(q, k, v):
    B, S, H, E = q.shape
    Sp = -(-S // ATTN_PAD_UNIT) * ATTN_PAD_UNIT
    pad = ((0, 0), (0, Sp - S), (0, 0), (0, 0))
    q, k, v = jnp.pad(q, pad), jnp.pad(k, pad), jnp.pad(v, pad)
    outs, maxes, sums = [], [], []
    for window, dilation in DILATED_PATTERNS:
        o, m, l = dilated_branch(q, k, v, window, dilation)
        outs.append(o)
        maxes.append(m)
        sums.append(l)
    ms, ls, os_ = jnp.stack(maxes), jnp.stack(sums), jnp.stack(outs)
    w = jnp.exp(ms - jnp.max(ms, axis=0, keepdims=True))
    den = jnp.sum(w * ls, axis=0)
    num = jnp.sum(w[..., None] * os_, axis=0)
    return (num / den[..., None])[:, :S]


def hgrn2_mixer(q, f, i, lb):
    B, S, H, K = q.shape
    V = i.shape[-1]
    C = HGRN_CHUNK
    nc = S // C
    qf = jax.nn.silu(q.astype(jnp.float32))
    forget = lb + (1.0 - lb) * jax.nn.sigmoid(f.astype(jnp.float32))
    key = 1.0 - forget
    log_f = jnp.log(forget)
    iv = i.astype(jnp.float32)
    qf = qf.reshape(B, nc, C, H, K)
    key = key.reshape(B, nc, C, H, K)
    log_f = log_f.reshape(B, nc, C, H, K)
    iv = iv.reshape(B, nc, C, H, V)
    b = jnp.cumsum(log_f, axis=2)
    q_dec = qf * jnp.exp(b)
    k_inv = key * jnp.exp(-b)
    A = jnp.einsum('bnthk,bnshk->bnhts', q_dec, k_inv)
    causal = jnp.tril(jnp.ones((C, C), dtype=bool))
    A = jnp.where(causal, A, 0.0)
    o_intra = jnp.einsum('bnhts,bnshv->bnthv', A, iv)
    b_end = b[:, :, -1]
    k_end = key * jnp.exp(b_end[:, :, None] - b)
    U = jnp.einsum('bnshk,bnshv->bnhkv', k_end, iv)
    decay = jnp.exp(b_end)

    def step(state, xs):
        d, u = xs
        return d[..., None] * state + u, state

    init = jnp.zeros((B, H, K, V), jnp.float32)
    _, states = lax.scan(step, init, (decay.transpose(1, 0, 2, 3), U.transpose(1, 0, 2, 3, 4)))
    states = states.transpose(1, 0, 2, 3, 4)
    o_inter = jnp.einsum('bnthk,bnhkv->bnthv', q_dec, states)
    return (o_intra + o_inter).reshape(B, S, H, V)


def conv_glu(u, w_up, conv_w, conv_b, w_down):
    S = u.shape[1]
    gate, val = jnp.split(u @ w_up, 2, axis=-1)
    gp = jnp.pad(gate, ((0, 0), (CONV_WIDTH - 1, 0), (0, 0)))
    conv = conv_b
    for j in range(CONV_WIDTH):
        conv = conv + conv_w[j] * gp[:, j:j + S]
    return (jax.nn.gelu(conv, approximate=False) * val) @ w_down


def setup_inputs(seed: int = 0) -> dict:
    key = jax.random.key(seed)
    ks = jax.random.split(key, 13)
    f32 = jnp.float32
    nrm = lambda k, shape: jax.random.normal(k, shape, f32)
    return {
        'x': nrm(ks[0], (BATCH, SEQ, D_MODEL)),
        'norm1_g': 1.0 + 0.02 * nrm(ks[1], (DEPTH, D_MODEL)),
        'w_in': nrm(ks[2], (DEPTH, D_MODEL, IN_TOTAL)) * D_MODEL ** -0.5,
        'attn_norm_g': 1.0 + 0.02 * nrm(ks[3], (DEPTH, ATTN_WIDTH)),
        'hgrn_norm_g': 1.0 + 0.02 * nrm(ks[4], (DEPTH, HGRN_WIDTH)),
        'hgrn_lb_logits': 0.1 * nrm(ks[5], (DEPTH + 1, HGRN_WIDTH)),
        'w_out': nrm(ks[6], (DEPTH, MIX_WIDTH, D_MODEL)) * MIX_WIDTH ** -0.5,
        'norm2_g': 1.0 + 0.02 * nrm(ks[7], (DEPTH, D_MODEL)),
        'w_up': nrm(ks[8], (DEPTH, D_MODEL, 2 * D_FF)) * D_MODEL ** -0.5,
        'conv_w': nrm(ks[9], (DEPTH, CONV_WIDTH, D_FF)) * CONV_WIDTH ** -0.5,
        'conv_b': 0.02 * nrm(ks[10], (DEPTH, D_FF)),
        'w_down': nrm(ks[11], (DEPTH, D_FF, D_MODEL)) * D_FF ** -0.5,
        'final_norm_g': 1.0 + 0.02 * nrm(ks[12], (D_MODEL,)),
    }


def reference(x, norm1_g, w_in, attn_norm_g, hgrn_norm_g, hgrn_lb_logits, w_out,
              norm2_g, w_up, conv_w, conv_b, w_down, final_norm_g):
    B, S, _ = x.shape
    lower_bounds = jnp.cumsum(jax.nn.softmax(hgrn_lb_logits.astype(jnp.float32), axis=0), axis=0)
    h = x
    for layer in range(DEPTH):
        u = rms_norm(h, norm1_g[layer])
        proj = u @ w_in[layer]
        aq, ak, av, hq, hf, hi, hg = jnp.split(proj, IN_SPLITS, axis=-1)
        attn = dilated_attention(aq.reshape(B, S, ATTN_HEADS, ATTN_HEAD_DIM),
                                 ak.reshape(B, S, ATTN_HEADS, ATTN_HEAD_DIM),
                                 av.reshape(B, S, ATTN_HEADS, ATTN_HEAD_DIM))
        attn = rms_norm(attn.reshape(B, S, ATTN_WIDTH), attn_norm_g[layer])
        lb = lower_bounds[layer].reshape(HGRN_HEADS, HGRN_FDIM)
        rec = hgrn2_mixer(hq.reshape(B, S, HGRN_HEADS, HGRN_FDIM),
                          hf.reshape(B, S, HGRN_HEADS, HGRN_FDIM),
                          hi.reshape(B, S, HGRN_HEADS, HGRN_IDIM), lb)
        rec = rms_norm(rec, hgrn_norm_g[layer].reshape(HGRN_HEADS, HGRN_IDIM))
        rec = rec * jax.nn.silu(hg.astype(jnp.float32).reshape(B, S, HGRN_HEADS, HGRN_IDIM))
        mixed = jnp.concatenate([attn.astype(jnp.float32), rec.reshape(B, S, HGRN_WIDTH)], axis=-1)
        h = h + mixed.astype(h.dtype) @ w_out[layer]
        u = rms_norm(h, norm2_g[layer])
        h = h + conv_glu(u, w_up[layer], conv_w[layer], conv_b[layer], w_down[layer]).astype(h.dtype)
    return rms_norm(h, final_norm_g)
```

```python
import numpy as np
from contextlib import ExitStack
import concourse.bass as bass
import concourse.mybir as mybir
from concourse.bass_utils import run_bass_kernel_spmd

F32 = mybir.dt.float32
BF16 = mybir.dt.bfloat16
AF = mybir.ActivationFunctionType
ALU = mybir.AluOpType

P = 128
T = 4096
D = 1024
KC = 8
NT = T // P
DFF = 2816
NFC = DFF // P
EPS = 1e-6
NEG = -30000.0
NCORES = 8


class Ev:
    __slots__ = ("sem", "val", "pe")

    def __init__(self, sem, val, pe=False):
        self.sem, self.val, self.pe = sem, val, pe


class Buf:
    __slots__ = ("name", "w", "r")

    def __init__(self, name=""):
        self.name, self.w, self.r = name, None, {}


class Eng:
    def __init__(self, name, sem, is_pe=False):
        self.name, self.sem, self.is_pe = name, sem, is_pe
        self.cnt = 0
        self.ops = []
        self.waited = {}
        self.pending = []


class DSem:
    def __init__(self, sem):
        self.sem, self.cnt = sem, 0
        self.nobarrier = False


class K:
    def __init__(self, nc, es):
        self.nc, self.es = nc, es
        self.eng = {}
        for n, pe in (("pe", True), ("act", False), ("dve", False), ("pool", False), ("sp", False)):
            self.eng[n] = Eng(n, es.enter_context(nc.semaphore("sem_" + n)), pe)
        self.nds = 0
        self.out_evs = []
        self.dsems = []
        self.mute = False
        self.marks = []

    def dsem(self):
        self.nds += 1
        d = DSem(self.es.enter_context(self.nc.semaphore("dsem%d" % self.nds)))
        self.dsems.append(d)
        return d

    def _wait(self, e, ev):
        if ev is None:
            return
        if e.is_pe and ev.pe:
            return
        assert ev.val is not None, "unresolved event"
        k = id(ev.sem)
        if e.waited.get(k, 0) >= ev.val:
            return
        e.waited[k] = ev.val
        e.ops.append(("w", ev.sem, ev.val))

    def _deps(self, e, reads, writes):
        for b in reads:
            self._wait(e, b.w)
        for b in writes:
            self._wait(e, b.w)
            for ev in b.r.values():
                self._wait(e, ev)

    def _mark(self, ev, reads, writes):
        for b in reads:
            b.r[id(ev.sem)] = ev
        for b in writes:
            b.w = ev
            b.r = {}

    def op(self, en, method, reads=(), writes=(), signal=True, **kw):
        if self.mute:
            return None
        e = self.eng[en]
        self._deps(e, reads, writes)
        ev = Ev(e.sem, None, e.is_pe)
        if signal:
            e.cnt += 1
            ev.val = e.cnt
            for p in e.pending:
                p.val = e.cnt
            e.pending = []
        else:
            assert e.is_pe
            e.pending.append(ev)
        e.ops.append(("op", method, kw, signal))
        self._mark(ev, reads, writes)
        return ev

    def dma(self, en, out, in_, ds, reads=(), writes=(), is_out=False):
        if self.mute:
            return None
        e = self.eng[en]
        self._deps(e, reads, writes)
        ds.cnt += 16
        ev = Ev(ds.sem, ds.cnt)
        e.ops.append(("dma", out, in_, ds.sem))
        self._mark(ev, reads, writes)
        if is_out:
            self.out_evs.append(ev)
        return ev

    def mark(self, name):
        self.marks.append((name, {n: sum(1 for o in e.ops if o[0] == 'op') for n, e in self.eng.items()}))

    def barrier(self):
        if self.mute:
            return
        names = ("pe", "act", "dve", "pool")
        pe = self.eng["pe"]
        if pe.pending:
            pe.cnt += 1
            for p in pe.pending:
                p.val = pe.cnt
            pe.pending = []
            pe.ops.append(("sig",))
        for en in names + ("sp",):
            e = self.eng[en]
            for on in names:
                o = self.eng[on]
                if o is e or o.cnt == 0:
                    continue
                if e.waited.get(id(o.sem), 0) < o.cnt:
                    e.waited[id(o.sem)] = o.cnt
                    e.ops.append(("w", o.sem, o.cnt))
            for ds in self.dsems:
                if ds.nobarrier:
                    continue
                if ds.cnt and e.waited.get(id(ds.sem), 0) < ds.cnt:
                    e.waited[id(ds.sem)] = ds.cnt
                    e.ops.append(("w", ds.sem, ds.cnt))

    def replay(self, en, h):
        e = self.eng[en]
        for o in e.ops:
            if o[0] == "w":
                h.wait_ge(o[1], o[2])
            elif o[0] == "sig":
                h.drain().then_inc(e.sem, 1)
            elif o[0] == "op":
                ins = getattr(h, o[1])(**o[2])
                if o[3]:
                    ins.then_inc(e.sem, 1)
            else:
                h.dma_start(out=o[1], in_=o[2]).then_inc(o[3], 16)


class Stop(Exception):
    pass


def build(debug=False, stop=None):
    nc = bass.Bass("TRN2", target_bir_lowering=False)
    x_d = nc.dram_tensor("x", [T, D], F32, kind="ExternalInput").ap()
    win_d = nc.dram_tensor("w_in", [D, 3584], F32, kind="ExternalInput").ap()
    wout_d = nc.dram_tensor("w_out", [D, D], F32, kind="ExternalInput").ap()
    wup_d = nc.dram_tensor("w_up", [D, 2 * DFF], F32, kind="ExternalInput").ap()
    wdn_d = nc.dram_tensor("w_down", [DFF, D], F32, kind="ExternalInput").ap()
    pv_d = nc.dram_tensor("pvec", [P, 128], F32, kind="ExternalInput").ap()
    gf_d = nc.dram_tensor("gfin", [D], F32, kind="ExternalInput").ap()
    out_d = nc.dram_tensor("out", [T, D], F32, kind="ExternalOutput").ap()
    mixed_d = nc.dram_tensor("mixed_scr", [D, T], BF16, kind="ExternalOutput" if debug else "Internal").ap()
    h_d = nc.dram_tensor("h_scr", [T, D], F32, kind="ExternalOutput" if debug else "Internal").ap()
    C_G1, C_GMIX, C_G2, C_LBL, C_CW, C_CB = 0, 8, 16, 24, 32, 98

    with ExitStack() as es:
        k = K(nc, es)
        sb = lambda n, s, d, st=es: st.enter_context(nc.sbuf_tensor(n, s, d))
        ps = [es.enter_context(nc.psum_tensor("ps%d" % i, [P, 512], F32)) for i in range(8)]
        psB = [Buf("ps%d" % i) for i in range(8)]
        ident = sb("ident", [P, P], BF16)
        identf = sb("identf", [P, P], F32)
        zf = sb("zf", [P, 256], F32)
        hmaskf = sb("hmaskf", [P, P], F32)
        hmask = sb("hmask", [P, P], BF16)
        ones = sb("ones", [P, P], BF16)
        maskp = sb("maskp", [P, 2, P], BF16)
        resetm = sb("resetm", [P, 1024], F32)
        pv = sb("pv", [P, 128], F32)
        lbt = sb("lbt", [P, 16], F32)
        nhalf = sb("nhalf", [P, 1], F32)
        gF = sb("gF", [P, D], F32)
        cB = Buf("consts")
        mixB = [Buf("mix%d" % i) for i in range(8)]
        pvB = Buf("pv")
        gFB = Buf("gF")

        k.dma("sp", pv[:], pv_d[:, :], k.dsem(), writes=[pvB])
        k.dma("sp", gF[:], gf_d.partition_broadcast(P), k.dsem(), writes=[gFB])
        k.op("pool", "memset", writes=[cB], ap=identf[:], constant=1.0)
        k.op("pool", "affine_select", reads=[cB], writes=[cB], out=identf[:], in_=identf[:], pattern=[[-1, P]],
             compare_op=ALU.is_equal, fill=0.0, base=0, channel_multiplier=1)
        k.op("pool", "tensor_copy", reads=[cB], writes=[cB], out=ident[:], in_=identf[:])
        k.op("pool", "memset", reads=[cB], writes=[cB], ap=zf[:], constant=1.0)
        k.op("pool", "memset", reads=[cB], writes=[cB], ap=hmaskf[:], constant=1.0)
        k.op("pool", "affine_select", reads=[cB], writes=[cB], out=hmaskf[:], in_=hmaskf[:], pattern=[[1, P]],
             compare_op=ALU.is_ge, fill=0.0, base=0, channel_multiplier=-1)
        k.op("pool", "memset", reads=[cB], writes=[cB], ap=hmaskf[0:64, 64:128], constant=0.0)
        k.op("pool", "tensor_copy", reads=[cB], writes=[cB], out=hmask[:], in_=hmaskf[:])
        k.op("pool", "memset", reads=[cB], writes=[cB], ap=ones[:], constant=1.0)
        k.op("pool", "affine_select", reads=[cB], writes=[cB], out=maskp[:], in_=zf[:].rearrange("p (h q) -> p h q", h=2),
             pattern=[[0, 2], [-1, P]],
             compare_op=ALU.is_ge, fill=0.0, base=0, channel_multiplier=1)
        k.op("pool", "memset", reads=[cB], writes=[cB], ap=resetm[:], constant=1.0)
        k.op("pool", "memset", reads=[cB], writes=[cB],
             ap=resetm[:].rearrange("p (c k) -> p c k", k=64)[:, :, 0:1], constant=0.0)
        k.op("pool", "memset", reads=[cB], writes=[cB], ap=nhalf[:], constant=-0.5)
        k.op("dve", "tensor_tensor", reads=[pvB], writes=[cB], out=lbt[:, 8:12], in0=pv[:, C_LBL:C_LBL + 4],
             in1=pv[:, C_LBL + 4:C_LBL + 8], op=ALU.subtract)
        k.op("act", "activation", reads=[cB], writes=[cB], out=lbt[:, 0:4], in_=lbt[:, 8:12], func=AF.Sigmoid)
        k.op("dve", "tensor_scalar", reads=[cB], writes=[cB], out=lbt[:, 4:8], in0=lbt[:, 0:4], scalar1=-1.0,
             scalar2=1.0, op0=ALU.mult, op1=ALU.add)

        def rstd_chain(ss_ap, ssn_ap, rstd_ap, n, bufs_r, buf_w, src_psum=False):
            k.op("dve", "tensor_scalar", reads=bufs_r, writes=[buf_w], out=ssn_ap, in0=ss_ap, scalar1=1.0 / n,
                 scalar2=EPS, op0=ALU.mult, op1=ALU.add)
            k.op("pool", "tensor_tensor", reads=[buf_w, cB], writes=[buf_w], out=rstd_ap, in0=ssn_ap,
                 in1=nhalf[:, 0:1], op=ALU.pow)

        try:
            with ExitStack() as ms:
                msb = lambda n, s, d: sb(n, s, d, ms)
                xT = msb("xT", [P, KC, T], BF16)
                xTB = [Buf("xT%d" % i) for i in range(NT)]
                wst = [msb("wst%d" % i, [P, KC, P], F32) for i in range(2)]
                wstB = [Buf() for _ in range(2)]
                wstD = [k.dsem() for _ in range(2)]
                wbf = [[msb("wbf%d_%d" % (g, j), [P, KC, P], BF16) for j in range(4)] for g in range(2)]
                wbfB = [[Buf() for _ in range(4)] for _ in range(2)]
                win_v = win_d.rearrange("(kc p) n -> p kc n", p=P)
                wctr = [0]

                def load_wblock(gslot, j, col0):
                    s = wctr[0] % 2
                    wctr[0] += 1
                    k.dma("sp", wst[s][:], win_v[:, :, col0:col0 + P], wstD[s], writes=[wstB[s]])
                    for kc in range(KC):
                        k.op("pool", "tensor_scalar", reads=[wstB[s], pvB], writes=[wbfB[gslot][j]],
                             out=wbf[gslot][j][:, kc, :], in0=wst[s][:, kc, :], scalar1=pv[:, C_G1 + kc:C_G1 + kc + 1],
                             scalar2=1.0, op0=ALU.mult, op1=ALU.mult)

                groups = []
                for hp in range(4):
                    groups.append(("attn", hp, [128 * hp, 512 + 128 * hp, 1024 + 128 * hp]))
                for hh in range(4):
                    groups.append(("hgrn", hh, [1536 + 128 * hh, 2048 + 128 * hh, 2560 + 128 * hh, 3072 + 128 * hh]))

                def load_group(gi):
                    if gi >= len(groups):
                        return
                    for j, c0 in enumerate(groups[gi][2]):
                        load_wblock(gi % 2, j, c0)

                load_group(0)

                with ExitStack() as pa:
                    NXS = 8
                    xin = [sb("xin%d" % i, [P, D], F32, pa) for i in range(NXS)]
                    xinB = [Buf() for _ in range(NXS)]
                    xinD = [k.dsem() for _ in range(NXS)]
                    junk = [sb("junk%d" % i, [P, D], BF16, pa) for i in range(2)]
                    junkB = [Buf() for _ in range(2)]
                    xb = [sb("xb%d" % i, [P, D], BF16, pa) for i in range(2)]
                    xbB = [Buf() for _ in range(2)]
                    st = sb("stA", [P, 3 * NT], F32, pa)
                    stB = [Buf() for _ in range(NT)]
                    for i in range(NT):
                        s = i % NXS
                        k.dma("sp", xin[s][:], x_d[i * P:(i + 1) * P, :], xinD[s], writes=[xinB[s]])
                        k.op("act", "activation", reads=[xinB[s]], writes=[junkB[i % 2], stB[i]], out=junk[i % 2][:],
                             in_=xin[s][:], func=AF.Square, accum_out=st[:, i:i + 1])
                        rstd_chain(st[:, i:i + 1], st[:, NT + i:NT + i + 1], st[:, 2 * NT + i:2 * NT + i + 1], D,
                                   [stB[i]], stB[i])
                        k.op("dve", "tensor_scalar", reads=[xinB[s], stB[i]], writes=[xbB[i % 2]], out=xb[i % 2][:],
                             in0=xin[s][:], scalar1=st[:, 2 * NT + i:2 * NT + i + 1], scalar2=None, op0=ALU.mult)
                        bp = (i % 4) * 2
                        for kc in range(KC):
                            bkk = bp + kc // 4
                            k.op("pe", "matmul", reads=[xbB[i % 2], cB], writes=[psB[bkk]], signal=(kc % 4 == 3),
                                 out=ps[bkk][:, (kc % 4) * P:(kc % 4 + 1) * P], lhsT=xb[i % 2][:, kc * P:(kc + 1) * P],
                                 rhs=ident[:], start=True, stop=True, skip_group_check=True)
                        k.op("act", "activation", reads=[psB[bp]], writes=[xTB[i]], out=xT[:, 0:4, i * P:(i + 1) * P],
                             in_=ps[bp][:].rearrange("p (c t) -> p c t", t=P), func=AF.Copy)
                        k.op("dve", "tensor_copy", reads=[psB[bp + 1]], writes=[xTB[i]], out=xT[:, 4:8, i * P:(i + 1) * P],
                             in_=ps[bp + 1][:].rearrange("p (c t) -> p c t", t=P))
                k.barrier()
                k.mark('A')
                if stop == 'A':
                    k.mute = True
                def proj_fm(wt, wB, c, bank, tok0=None, ntok=512):
                    t0 = c * 512 if tok0 is None else tok0
                    tiles = sorted(set(range(t0 // P, (t0 + ntok - 1) // P + 1)))
                    for kc in range(KC):
                        k.op("pe", "matmul", reads=[wB] + [xTB[t] for t in tiles], writes=[psB[bank]],
                             signal=(kc == KC - 1), out=ps[bank][:, 0:ntok], lhsT=wt[:, kc, :],
                             rhs=xT[:, kc, t0:t0 + ntok], start=(kc == 0), stop=(kc == KC - 1))

                with ExitStack() as pb:
                    bsb = lambda n, s, d: sb(n, s, d, pb)
                    qT2 = bsb("qT2", [P, 2, T], BF16)
                    qzB = Buf("qzero")
                    k.op("pool", "memset", writes=[qzB], ap=qT2[64:128, 0, :], constant=0.0)
                    k.op("pool", "memset", writes=[qzB], ap=qT2[0:64, 1, :], constant=0.0)
                    kT = bsb("kT", [P, T], BF16)
                    VT = bsb("VT", [P, T], BF16)
                    VTB = [Buf() for _ in range(8)]
                    qB = [Buf() for _ in range(8)]
                    kB = [Buf() for _ in range(8)]
                    DILS = (1, 4, 16)
                    V = [bsb("V%d" % di, [P, NT, P], BF16) for di in range(3)]
                    VB = [[Buf() for _ in range(NT)] for _ in range(3)]
                    pT = [bsb("pT%d" % i, [P, 2, 256], BF16) for i in range(4)]
                    pTB = [Buf() for _ in range(4)]
                    pTBp = [Buf() for _ in range(4)]
                    Oacc = bsb("Oacc", [P, T], F32)
                    Lacc = bsb("Lacc", [P, T], F32)
                    OB = [Buf() for _ in range(8)]
                    LB_ = [Buf() for _ in range(8)]
                    rl = [bsb("rl%d" % i, [P, 512], F32) for i in range(2)]
                    rlB = [Buf() for _ in range(2)]
                    at = [bsb("at%d" % i, [P, 512], BF16) for i in range(2)]
                    atB = [Buf() for _ in range(2)]
                    atD = [k.dsem() for _ in range(2)]
                    SB_BANKS = (0, 1, 6)
                    O_BANKS = ((2, 3), (4, 5))
                    PJ_BANKS = (0, 1)
                    sctr = [0]
                    pjc = [0]
                    for gi in range(4):
                        hp = groups[gi][1]
                        gs = gi % 2
                        load_group(gi + 1)
                        wq, wk, wv = wbf[gs][0], wbf[gs][1], wbf[gs][2]
                        for c in range(8):
                            for which in ("q", "k"):
                                bank = PJ_BANKS[pjc[0] % 2]
                                pjc[0] += 1
                                if which == "q":
                                    proj_fm(wq, wbfB[gs][0], c, bank)
                                    k.op("act", "activation", reads=[psB[bank]], writes=[qB[c]],
                                         out=qT2[0:64, 0, c * 512:(c + 1) * 512], in_=ps[bank][0:64, :], func=AF.Copy, scale=0.125)
                                    k.op("dve", "tensor_scalar", reads=[psB[bank]], writes=[qB[c]],
                                         out=qT2[64:128, 1, c * 512:(c + 1) * 512], in0=ps[bank][64:128, :], scalar1=0.125,
                                         scalar2=None, op0=ALU.mult)
                                else:
                                    proj_fm(wk, wbfB[gs][1], c, bank)
                                    if c % 2:
                                        k.op("act", "activation", reads=[psB[bank]], writes=[kB[c]],
                                             out=kT[:, c * 512:(c + 1) * 512], in_=ps[bank][:], func=AF.Copy)
                                    else:
                                        k.op("dve", "tensor_copy", reads=[psB[bank]], writes=[kB[c]],
                                             out=kT[:, c * 512:(c + 1) * 512], in_=ps[bank][:])
                        for c in range(8):
                            bank = PJ_BANKS[pjc[0] % 2]
                            pjc[0] += 1
                            proj_fm(wv, wbfB[gs][2], c, bank)
                            if c % 2:
                                k.op("act", "activation", reads=[psB[bank]], writes=[VTB[c]],
                                     out=VT[:, c * 512:(c + 1) * 512], in_=ps[bank][:], func=AF.Copy)
                            else:
                                k.op("dve", "tensor_copy", reads=[psB[bank]], writes=[VTB[c]],
                                     out=VT[:, c * 512:(c + 1) * 512], in_=ps[bank][:])
                        for di, d in enumerate(DILS):
                            nb = NT // d
                            for b0 in range(0, NT, 4):
                                bank = PJ_BANKS[pjc[0] % 2]
                                pjc[0] += 1
                                for j in range(4):
                                    b = b0 + j
                                    r, n = b // nb, b % nb
                                    t0 = n * P * d
                                    chs = sorted(set(t // 4 for t in range(t0 // P, t0 // P + d)))
                                    lt = VT[:, t0:t0 + P * d].rearrange("p (i s) -> p i s", s=d)[:, :, r]
                                    k.op("pe", "matmul", reads=[cB] + [VTB[c] for c in chs], writes=[psB[bank]], signal=(j == 3),
                                         out=ps[bank][:, j * P:(j + 1) * P], lhsT=lt, rhs=ident[:], start=True, stop=True,
                                         skip_group_check=True)
                                eng = "act" if pjc[0] % 2 else "dve"
                                kw = {"func": AF.Copy} if eng == "act" else {}
                                k.op(eng, "activation" if eng == "act" else "tensor_copy", reads=[psB[bank]],
                                     writes=[VB[di][b0 + j] for j in range(4)], out=V[di][:, b0:b0 + 4, :],
                                     in_=ps[bank][:].rearrange("p (j f) -> p j f", f=P), **kw)
                        fresh = [True, True]

                        def stage1(di, d, b):
                            nb = NT // d
                            r, n = b // nb, b % nb
                            has_next = (n + 1 < nb)
                            nq = 256 if has_next else 128
                            t0 = n * P * d
                            sbk = SB_BANKS[sctr[0] % 3]
                            pslot = sctr[0] % 4
                            sctr[0] += 1
                            sview = ps[sbk][:, 0:2 * nq].rearrange("p (h q) -> p h q", h=2)
                            ktiles = list(range(t0 // P, t0 // P + d))
                            qtiles = list(range(t0 // P, min(NT, t0 // P + (2 if has_next else 1) * d)))
                            kchunks = sorted(set(t // 4 for t in ktiles))
                            qchunks = sorted(set(t // 4 for t in qtiles))
                            lt = kT[:, t0:t0 + P * d].rearrange("p (i s) -> p i s", s=d)[:, :, r]
                            rt = qT2[:, :, t0:t0 + nq * d].rearrange("p h (i s) -> p h i s", s=d)[:, :, :, r]
                            k.op("pe", "matmul", reads=[kB[c] for c in kchunks] + [qB[c] for c in qchunks] + [qzB],
                                 writes=[psB[sbk]], out=sview, lhsT=lt, rhs=rt, start=True, stop=True)
                            k.op("act", "activation", reads=[psB[sbk]], writes=[pTB[pslot]] + ([pTBp[pslot]] if has_next else []),
                                 out=pT[pslot][:, :, 0:nq], in_=sview, func=AF.Exp)
                            k.op("pool", "affine_select", reads=[pTB[pslot]], writes=[pTB[pslot]], out=pT[pslot][:, :, 0:P],
                                 in_=pT[pslot][:, :, 0:P], pattern=[[0, 2], [1, P]], compare_op=ALU.is_ge, fill=0.0, base=0,
                                 channel_multiplier=-1)
                            if has_next:
                                k.op("dve", "tensor_tensor", reads=[pTBp[pslot], cB], writes=[pTBp[pslot]],
                                     out=pT[pslot][:, :, P:2 * P], in0=pT[pslot][:, :, P:2 * P], in1=maskp[:], op=ALU.mult)
                            return (di, d, b, has_next, pslot)

                        def stage2(item):
                            di, d, b, has_next, pslot = item
                            cch = b // 4
                            par = cch % 2
                            bn, bl = O_BANKS[par]
                            col = (b % 4) * P
                            parts = [(0, P, par, col)]
                            if has_next:
                                if b % 4 == 3:
                                    parts.append((P, P, 1 - par, 0))
                                else:
                                    parts[0] = (0, 256, par, col)
                            for pi, (pc0, wd, pr, oc) in enumerate(parts):
                                bn_, bl_ = O_BANKS[pr]
                                for h in range(2):
                                    hs_ = slice(h * 64, (h + 1) * 64)
                                    first = fresh[pr]
                                    prd = ([pTB[pslot]] if pc0 == 0 else []) + ([pTBp[pslot]] if pc0 + wd > P else [])
                                    k.op("pe", "matmul", reads=[VB[di][b]] + prd, writes=[psB[bn_]], signal=False,
                                         out=ps[bn_][hs_, oc:oc + wd], lhsT=V[di][:, b, hs_],
                                         rhs=pT[pslot][:, h, pc0:pc0 + wd], start=first, stop=True,
                                         skip_group_check=True)
                                    k.op("pe", "matmul", reads=[cB] + prd, writes=[psB[bl_]],
                                         signal=(h == 1), out=ps[bl_][hs_, oc:oc + wd], lhsT=ones[:, 0:64],
                                         rhs=pT[pslot][:, h, pc0:pc0 + wd], start=first, stop=True,
                                         skip_group_check=True)
                                fresh[pr] = False
                            if b % 4 == 3:
                                if d == 1:
                                    do, dl = Oacc[:, cch * 512:(cch + 1) * 512], Lacc[:, cch * 512:(cch + 1) * 512]
                                    so, sl = ps[bn][:], ps[bl][:]
                                    chs = [cch]
                                elif d == 4:
                                    rr, half = cch // 2, cch % 2
                                    v = lambda A: A[:, half * 2048:(half + 1) * 2048].rearrange("p (m s) -> p m s", s=4)[:, :, rr]
                                    do, dl = v(Oacc), v(Lacc)
                                    so, sl = ps[bn][:], ps[bl][:]
                                    chs = list(range(half * 4, half * 4 + 4))
                                else:
                                    v = lambda A: A[:, :].rearrange("p (m s) -> p m s", s=16)[:, :, 2 * cch:2 * cch + 2]
                                    do, dl = v(Oacc), v(Lacc)
                                    so = ps[bn][:].rearrange("p (r m) -> p m r", r=2)
                                    sl = ps[bl][:].rearrange("p (r m) -> p m r", r=2)
                                    chs = list(range(8))
                                if d == 1:
                                    k.op("dve", "tensor_copy", reads=[psB[bn]], writes=[OB[c] for c in chs], out=do, in_=so)
                                    k.op("act", "activation", reads=[psB[bl]], writes=[LB_[c] for c in chs], out=dl, in_=sl,
                                         func=AF.Copy)
                                else:
                                    k.op("dve", "tensor_tensor", reads=[psB[bn]], writes=[OB[c] for c in chs], out=do,
                                         in0=so, in1=do, op=ALU.add)
                                    k.op("dve", "tensor_tensor", reads=[psB[bl]], writes=[LB_[c] for c in chs], out=dl,
                                         in0=sl, in1=dl, op=ALU.add)
                                fresh[par] = True

                        work = [(di, d, b) for di, d in enumerate(DILS) for b in range(NT)]
                        q_items = [stage1(*work[0]), stage1(*work[1])]
                        for wi in range(len(work)):
                            if wi + 2 < len(work):
                                q_items.append(stage1(*work[wi + 2]))
                            stage2(q_items.pop(0))
                        for c in range(8):
                            k.op("act", "activation", reads=[LB_[c]], writes=[LB_[c]], out=Lacc[:, c * 512:(c + 1) * 512],
                                 in_=Lacc[:, c * 512:(c + 1) * 512], func=AF.Ln)
                        for c in range(8):
                            s = c % 2
                            k.op("act", "activation", reads=[LB_[c]], writes=[rlB[s]], out=rl[s][:],
                                 in_=Lacc[:, c * 512:(c + 1) * 512], func=AF.Exp, scale=-1.0)
                            k.op("dve", "tensor_tensor", reads=[OB[c], rlB[s]], writes=[atB[s]], out=at[s][:],
                                 in0=Oacc[:, c * 512:(c + 1) * 512], in1=rl[s][:], op=ALU.mult)
                            k.dma("pool", mixed_d[hp * P:(hp + 1) * P, c * 512:(c + 1) * 512], at[s][:], atD[s],
                                  reads=[atB[s]], writes=[mixB[hp]])
                k.barrier()
                k.mark('B')
                if stop == 'B':
                    k.mute = True
                with ExitStack() as pc:
                    csb = lambda n, s, d: sb(n, s, d, pc)
                    SEG = 1024
                    f32b = {}
                    for nm in ("SG", "SQ", "QF", "S3", "FG", "KY", "LF", "Bc", "ENB", "BD", "EKD", "REC", "LN1", "RS", "R1"):
                        f32b[nm] = (csb("h_" + nm, [P, SEG], F32), Buf(nm))
                    A = lambda nm: f32b[nm][0]
                    Bf = lambda nm: f32b[nm][1]
                    dbl = {}
                    for nm, dt_ in (("QD", BF16), ("KI", BF16), ("KE", BF16), ("EB", F32), ("GT", F32)):
                        dbl[nm] = ([csb("h_%s%d" % (nm, i), [P, SEG], dt_) for i in range(2)], [Buf(nm) for _ in range(2)])
                    RSQ, RSQB = csb("h_RSQ", [P, SEG], BF16), Buf("RSQ")
                    MX = [csb("h_MX%d" % i, [P, SEG], BF16) for i in range(2)]
                    MXB = [Buf() for _ in range(2)]
                    MXD = [k.dsem() for _ in range(2)]
                    IVs = [csb("h_IV%d" % i, [P, 8, P], BF16) for i in range(2)]
                    IVBs = [[Buf() for _ in range(2)] for _ in range(2)]
                    Ke8 = csb("h_Ke8", [P, 8, P], BF16)
                    Ke8B = [Buf() for _ in range(8)]
                    As8 = csb("h_As8", [P, 8, P], BF16)
                    As8B = [Buf() for _ in range(8)]
                    Sf = [csb("h_Sf%d" % i, [P, P], F32) for i in range(2)]
                    SfB = [Buf() for _ in range(2)]
                    S16 = csb("h_S16", [P, 17, P], BF16)
                    S16B = [Buf() for _ in range(17)]
                    mxc = [0]
                    sidx = [0]
                    pjb = [0]

                    def stageA(gi, hh, sg, sl):
                        gs = gi % 2
                        if sg == 0:
                            load_group(gi + 1)
                        wq, wf, wi, wg = wbf[gs]
                        wqB, wfB, wiB, wgB = wbfB[gs]
                        lbc, omlc = lbt[:, hh:hh + 1], lbt[:, 4 + hh:5 + hh]
                        QD, KI, KE, EB, GT = (dbl[n][0][sl] for n in ("QD", "KI", "KE", "EB", "GT"))
                        QDB, KIB, KEB, EBB, GTB = (dbl[n][1][sl] for n in ("QD", "KI", "KE", "EB", "GT"))
                        IV, IVB = IVs[sl], IVBs[sl]

                        def nbank():
                            pjb[0] += 1
                            return (0, 1, 6, 7)[pjb[0] % 4]
                        for cc in range(2):
                            cs = slice(cc * 512, (cc + 1) * 512)
                            bk = nbank()
                            proj_fm(wf, wfB, sg * 2 + cc, bk)
                            k.op("act", "activation", reads=[psB[bk]], writes=[Bf("SG")], out=A("SG")[:, cs], in_=ps[bk][:],
                                 func=AF.Sigmoid)
                            yield
                        for cc in range(2):
                            cs = slice(cc * 512, (cc + 1) * 512)
                            bk = nbank()
                            proj_fm(wq, wqB, sg * 2 + cc, bk)
                            k.op("act", "activation", reads=[psB[bk]], writes=[Bf("QF")], out=A("QF")[:, cs], in_=ps[bk][:],
                                 func=AF.Silu)
                            yield
                        for cc in range(2):
                            cs = slice(cc * 512, (cc + 1) * 512)
                            bk = nbank()
                            proj_fm(wg, wgB, sg * 2 + cc, bk)
                            k.op("act", "activation", reads=[psB[bk]], writes=[GTB], out=GT[:, cs], in_=ps[bk][:],
                                 func=AF.Silu)
                            yield
                        for half in range(2):
                            bank = nbank()
                            for j in range(4):
                                ti = sg * 8 + half * 4 + j
                                for kc in range(KC):
                                    k.op("pe", "matmul", reads=[wiB, xTB[ti]], writes=[psB[bank]],
                                         signal=(kc == KC - 1 and j == 3), out=ps[bank][:, j * P:(j + 1) * P],
                                         lhsT=xT[:, kc, ti * P:(ti + 1) * P], rhs=wi[:, kc, :], start=(kc == 0),
                                         stop=(kc == KC - 1))
                            k.op("act", "activation", reads=[psB[bank]], writes=[IVB[half]],
                                 out=IV[:, half * 4:(half + 1) * 4, :], in_=ps[bank][:].rearrange("p (j f) -> p j f", f=P),
                                 func=AF.Copy)
                            yield
                        k.op("dve", "tensor_scalar", reads=[Bf("SG"), cB], writes=[Bf("FG")], out=A("FG")[:], in0=A("SG")[:],
                             scalar1=omlc, scalar2=lbc, op0=ALU.mult, op1=ALU.add)
                        yield
                        k.op("dve", "tensor_scalar", reads=[Bf("FG")], writes=[Bf("KY")], out=A("KY")[:], in0=A("FG")[:],
                             scalar1=-1.0, scalar2=1.0, op0=ALU.mult, op1=ALU.add)
                        yield
                        k.op("act", "activation", reads=[Bf("FG")], writes=[Bf("LF")], out=A("LF")[:], in_=A("FG")[:], func=AF.Ln)
                        yield
                        k.op("dve", "tensor_tensor_scan", reads=[Bf("LF"), cB], writes=[Bf("Bc")], out=A("Bc")[:],
                             data0=resetm[:], data1=A("LF")[:], initial=0.0, op0=ALU.mult, op1=ALU.add)
                        yield
                        k.op("act", "activation", reads=[Bf("Bc")], writes=[EBB], out=EB[:], in_=A("Bc")[:], func=AF.Exp)
                        yield
                        k.op("act", "activation", reads=[Bf("Bc")], writes=[Bf("ENB")], out=A("ENB")[:], in_=A("Bc")[:],
                             func=AF.Exp, scale=-1.0)
                        yield
                        bv = A("Bc")[:].rearrange("p (c k) -> p c k", k=64)
                        k.op("dve", "tensor_tensor", reads=[Bf("Bc")], writes=[Bf("BD")],
                             out=A("BD")[:].rearrange("p (c k) -> p c k", k=64),
                             in0=bv[:, :, 63:64].to_broadcast([P, SEG // 64, 64]), in1=bv, op=ALU.subtract)
                        yield
                        k.op("act", "activation", reads=[Bf("BD")], writes=[Bf("EKD")], out=A("EKD")[:], in_=A("BD")[:], func=AF.Exp)
                        yield
                        k.op("dve", "tensor_tensor", reads=[Bf("QF"), EBB], writes=[QDB], out=QD[:], in0=A("QF")[:],
                             in1=EB[:], op=ALU.mult)
                        yield
                        k.op("pool", "tensor_tensor", reads=[Bf("KY"), Bf("ENB")], writes=[KIB], out=KI[:], in0=A("KY")[:],
                             in1=A("ENB")[:], op=ALU.mult)
                        yield
                        k.op("pool", "tensor_tensor", reads=[Bf("KY"), Bf("EKD")], writes=[KEB], out=KE[:], in0=A("KY")[:],
                             in1=A("EKD")[:], op=ALU.mult)
                        yield

                    def stageB(gi, hh, sg, sl):
                        tk0 = sg * SEG
                        QD, KI, KE, EB, GT = (dbl[n][0][sl] for n in ("QD", "KI", "KE", "EB", "GT"))
                        QDB, KIB, KEB, EBB, GTB = (dbl[n][1][sl] for n in ("QD", "KI", "KE", "EB", "GT"))
                        IV, IVB = IVs[sl], IVBs[sl]
                        if sg == 0:
                            k.op("pool", "memset", writes=[SfB[0]], ap=Sf[0][:], constant=0.0)
                            k.op("pool", "memset", writes=[S16B[0]], ap=S16[:, 0, :], constant=0.0)
                            first_state = 0
                        else:
                            first_state = None
                        for tl in range(8):
                            ts_ = slice(tl * P, (tl + 1) * P)
                            a = tl % 2
                            k.op("pe", "matmul", reads=[KIB, QDB], writes=[psB[a]], out=ps[a][:, 0:P], lhsT=KI[:, ts_],
                                 rhs=QD[:, ts_], start=True, stop=True, skip_group_check=True)
                            k.op("pe", "matmul", reads=[KEB, cB], writes=[psB[6 + a]], out=ps[6 + a][:, 0:P], lhsT=KE[:, ts_],
                                 rhs=ident[:], start=True, stop=True, skip_group_check=True)
                            k.op("dve", "tensor_tensor", reads=[psB[a], cB], writes=[As8B[tl]], out=As8[:, tl, :], in0=ps[a][:, 0:P],
                                 in1=hmask[:], op=ALU.mult)
                            k.op("act", "activation", reads=[psB[6 + a]], writes=[Ke8B[tl]], out=Ke8[:, tl, :], in_=ps[6 + a][:, 0:P],
                                 func=AF.Copy)
                        for tl in range(8):
                            for c2 in range(2):
                                c = 2 * tl + c2
                                ub = 2 + c % 4
                                k.op("pe", "matmul", reads=[Ke8B[tl], IVB[tl // 4]], writes=[psB[ub]],
                                     out=ps[ub][:, (c // 4) * P:(c // 4 + 1) * P], lhsT=Ke8[c2 * 64:(c2 + 1) * 64, tl, :],
                                     rhs=IV[c2 * 64:(c2 + 1) * 64, tl, :], start=True, stop=True, skip_group_check=True)
                        if sg != 0:
                            k.op("dve", "tensor_copy", reads=[S16B[16]], writes=[S16B[0]], out=S16[:, 0, :], in_=S16[:, 16, :])
                        for c in range(16):
                            cur, nxt = sidx[0] % 2, (sidx[0] + 1) % 2
                            sidx[0] += 1
                            ub = 2 + c % 4
                            ecol = c * 64 + 63
                            k.op("dve", "scalar_tensor_tensor", reads=[SfB[cur], EBB, psB[ub]], writes=[SfB[nxt]],
                                 out=Sf[nxt][:], in0=Sf[cur][:], scalar=EB[:, ecol:ecol + 1],
                                 in1=ps[ub][:, (c // 4) * P:(c // 4 + 1) * P], op0=ALU.mult, op1=ALU.add)
                            k.op("dve", "tensor_copy", reads=[SfB[nxt]], writes=[S16B[c + 1]], out=S16[:, c + 1, :], in_=Sf[nxt][:])
                    def stageB2(gi, hh, sg, sl):
                        tk0 = sg * SEG
                        QD, KI, KE, EB, GT = (dbl[n][0][sl] for n in ("QD", "KI", "KE", "EB", "GT"))
                        QDB, KIB, KEB, EBB, GTB = (dbl[n][1][sl] for n in ("QD", "KI", "KE", "EB", "GT"))
                        IV, IVB = IVs[sl], IVBs[sl]
                        for tl in range(8):
                            ob = 6 + tl // 4
                            oc = (tl % 4) * P
                            k.op("pe", "matmul", reads=[IVB[tl // 4], As8B[tl]], writes=[psB[ob]], signal=False,
                                 out=ps[ob][:, oc:oc + P], lhsT=IV[:, tl, :], rhs=As8[:, tl, :], start=(tl % 4 == 0), stop=False,
                                 skip_group_check=True)
                            for c2 in range(2):
                                c = 2 * tl + c2
                                k.op("pe", "matmul", reads=[S16B[c], QDB], writes=[psB[ob]], signal=(c2 == 1),
                                     out=ps[ob][:, oc + c2 * 64:oc + (c2 + 1) * 64], lhsT=S16[:, c, :],
                                     rhs=QD[:, tl * P + c2 * 64:tl * P + (c2 + 1) * 64], start=False, stop=(c2 == 1),
                                     skip_group_check=True)
                        for hf in range(2):
                            k.op("act" if hf == 0 else "dve", "activation" if hf == 0 else "tensor_copy", reads=[psB[6 + hf]],
                                 writes=[Bf("REC")], out=A("REC")[:, hf * 512:(hf + 1) * 512], in_=ps[6 + hf][:],
                                 **({"func": AF.Copy} if hf == 0 else {}))
                        k.op("act", "activation", reads=[Bf("REC")], writes=[RSQB], out=RSQ[:], in_=A("REC")[:], func=AF.Square)
                        for cc in range(2):
                            cs = slice(cc * 512, (cc + 1) * 512)
                            k.op("pe", "matmul", reads=[RSQB, cB], writes=[psB[cc]], out=ps[cc][:], lhsT=ones[:], rhs=RSQ[:, cs],
                                 start=True, stop=True)
                            k.op("act", "activation", reads=[psB[cc]], writes=[Bf("LN1")], out=A("LN1")[:, cs], in_=ps[cc][:],
                                 func=AF.Ln, scale=1.0 / P, bias=EPS)
                        k.op("act", "activation", reads=[Bf("LN1")], writes=[Bf("RS")], out=A("RS")[:], in_=A("LN1")[:], func=AF.Exp,
                             scale=-0.5)
                        k.op("dve", "tensor_tensor", reads=[Bf("REC"), Bf("RS")], writes=[Bf("R1")], out=A("R1")[:], in0=A("REC")[:],
                             in1=A("RS")[:], op=ALU.mult)
                        m = mxc[0] % 2
                        mxc[0] += 1
                        k.op("dve", "tensor_tensor", reads=[Bf("R1"), GTB], writes=[MXB[m]], out=MX[m][:], in0=A("R1")[:],
                             in1=GT[:], op=ALU.mult)
                        k.dma("pool", mixed_d[512 + hh * P:512 + (hh + 1) * P, tk0:tk0 + SEG], MX[m][:], MXD[m], reads=[MXB[m]],
                              writes=[mixB[4 + hh]])

                    hwork = [(gi, groups[gi][1], sg) for gi in range(4, 8) for sg in range(T // SEG)]
                    for _ in stageA(*hwork[0], 0):
                        pass
                    for wi_ in range(len(hwork)):
                        stageB(*hwork[wi_], wi_ % 2)
                        if wi_ + 1 < len(hwork):
                            for _ in stageA(*hwork[wi_ + 1], (wi_ + 1) % 2):
                                pass
                        stageB2(*hwork[wi_], wi_ % 2)
            k.barrier()
            k.mark('C')
            if stop == 'C':
                k.mute = True
            with ExitStack() as fs:
                wup = sb("wup", [P, KC, 2 * DFF], BF16, fs)
                wdn = sb("wdn", [P, NFC, D], BF16, fs)
                wupB = [Buf() for _ in range(KC)]
                wdnB = [Buf() for _ in range(NFC)]
                cast_rr = [0]

                def cast(out, in_, scal, reads, writes):
                    e = ("pool", "act", "dve")[cast_rr[0] % 3]
                    cast_rr[0] += 1
                    if e == "pool":
                        if scal is None:
                            k.op("pool", "tensor_copy", reads=reads, writes=writes, out=out, in_=in_)
                        else:
                            k.op("pool", "tensor_scalar", reads=reads + [pvB], writes=writes, out=out, in0=in_, scalar1=scal,
                                 scalar2=1.0, op0=ALU.mult, op1=ALU.mult)
                    elif e == "act":
                        if scal is None:
                            k.op("act", "activation", reads=reads, writes=writes, out=out, in_=in_, func=AF.Copy)
                        else:
                            k.op("act", "activation", reads=reads + [pvB], writes=writes, out=out, in_=in_, func=AF.Copy,
                                 scale=scal)
                    else:
                        if scal is None:
                            k.op("dve", "tensor_copy", reads=reads, writes=writes, out=out, in_=in_)
                        else:
                            k.op("dve", "tensor_scalar", reads=reads + [pvB], writes=writes, out=out, in0=in_, scalar1=scal,
                                 scalar2=None, op0=ALU.mult)

                hB = [Buf("h%d" % i) for i in range(NT)]
                with ExitStack() as pd:
                    dsb = lambda n, s, d: sb(n, s, d, pd)
                    wo = dsb("wo", [P, KC, D], BF16)
                    woB = [Buf() for _ in range(KC)]
                    wstF = [dsb("wstF%d" % i, [P, 1408], F32) for i in range(2)]
                    wstFB = [Buf() for _ in range(2)]
                    wstFD = [k.dsem() for _ in range(2)]
                    wc = [0]

                    def stage(src_ap, ncols):
                        s = wc[0] % 2
                        wc[0] += 1
                        k.dma("sp", wstF[s][:, 0:ncols], src_ap, wstFD[s], writes=[wstFB[s]])
                        return s

                    for kc in range(KC):
                        s = stage(wout_d[kc * P:(kc + 1) * P, :], D)
                        cast(wo[:, kc, :], wstF[s][:, 0:D], pv[:, C_GMIX + kc:C_GMIX + kc + 1], [wstFB[s]], [woB[kc]])

                    def load_ffn_weights(step):
                        if step < 32:
                            kc, q = step // 4, step % 4
                            s = stage(wup_d[kc * P:(kc + 1) * P, q * 1408:(q + 1) * 1408], 1408)
                            cast(wup[:, kc, q * 1408:(q + 1) * 1408], wstF[s][:, 0:1408], pv[:, C_G2 + kc:C_G2 + kc + 1],
                                 [wstFB[s]], [wupB[kc]])

                    mxt = [dsb("mxt%d" % i, [P, KC, 512], BF16) for i in range(2)]
                    mxtB = [Buf() for _ in range(2)]
                    mxtD = [k.dsem() for _ in range(2)]
                    xin2 = [dsb("xin2_%d" % i, [P, D], F32) for i in range(2)]
                    xin2B = [Buf() for _ in range(2)]
                    xin2D = [k.dsem() for _ in range(2)]
                    hsb = [dsb("hsb%d" % i, [P, D], F32) for i in range(2)]
                    hsbB = [Buf() for _ in range(2)]
                    hsbD = [k.dsem() for _ in range(2)]
                    sq = [dsb("sq%d" % i, [P, 4, P], BF16) for i in range(2)]
                    sqB = [Buf() for _ in range(2)]
                    stD = dsb("stD", [P, 2 * NT], F32)
                    stDB = [Buf() for _ in range(NT)]
                    mixed_v = mixed_d.rearrange("(kc p) t -> p kc t", p=P)
                    wstep = 0
                    for i in range(NT):
                        c, tl = i // 4, i % 4
                        ms_ = c % 2
                        s = i % 2
                        if tl == 0:
                            k.dma("sp", mxt[ms_][:], mixed_v[:, :, c * 512:(c + 1) * 512], mxtD[ms_], reads=mixB,
                                  writes=[mxtB[ms_]])
                        k.dma("sp", xin2[s][:], x_d[i * P:(i + 1) * P, :], xin2D[s], writes=[xin2B[s]])
                        for _ in range(2):
                            load_ffn_weights(wstep)
                            wstep += 1
                        tsl = slice(tl * P, (tl + 1) * P)
                        k.op("pool", "tensor_tensor", reads=[mxtB[ms_]], writes=[sqB[s]], out=sq[s][:], in0=mxt[ms_][:, 0:4, tsl],
                             in1=mxt[ms_][:, 0:4, tsl], op=ALU.mult)
                        sbk = 4 + s
                        for kc in range(4):
                            k.op("pe", "matmul", reads=[sqB[s], cB], writes=[psB[sbk]], signal=(kc == 3), out=ps[sbk][:, 0:2],
                                 lhsT=sq[s][:, kc, :], rhs=ones[:, 0:2], start=(kc == 0), stop=(kc == 3))
                        rstd_chain(ps[sbk][:, 0:1], stD[:, i:i + 1], stD[:, NT + i:NT + i + 1], 512, [psB[sbk]], stDB[i])
                        for half in range(2):
                            ba, br = 2 * half, 2 * half + 1
                            hsl = slice(half * 512, (half + 1) * 512)
                            for kc in range(4):
                                k.op("pe", "matmul", reads=[mxtB[ms_], woB[kc]], writes=[psB[ba]], signal=(kc == 3), out=ps[ba][:],
                                     lhsT=mxt[ms_][:, kc, tsl], rhs=wo[:, kc, hsl], start=(kc == 0), stop=(kc == 3))
                            for kc in range(4, 8):
                                k.op("pe", "matmul", reads=[mxtB[ms_], woB[kc]], writes=[psB[br]], signal=(kc == 7), out=ps[br][:],
                                     lhsT=mxt[ms_][:, kc, tsl], rhs=wo[:, kc, hsl], start=(kc == 4), stop=(kc == 7))
                            k.op("dve", "tensor_tensor", reads=[psB[br], xin2B[s]], writes=[hsbB[s]], out=hsb[s][:, hsl], in0=ps[br][:],
                                 in1=xin2[s][:, hsl], op=ALU.add)
                            k.op("dve", "scalar_tensor_tensor", reads=[psB[ba], stDB[i]], writes=[hsbB[s]], out=hsb[s][:, hsl],
                                 in0=ps[ba][:], scalar=stD[:, NT + i:NT + i + 1], in1=hsb[s][:, hsl], op0=ALU.mult, op1=ALU.add)
                        k.dma("pool", h_d[i * P:(i + 1) * P, :], hsb[s][:], hsbD[s], reads=[hsbB[s]], writes=[hB[i]])
                    while wstep < 32:
                        load_ffn_weights(wstep)
                        wstep += 1
                    for g0 in range(0, NFC, 6):
                        wdnD = k.dsem()
                        wdnD.nobarrier = True
                        grp = list(range(g0, min(NFC, g0 + 6)))
                        for fc in grp:
                            ev = k.dma("pool", wdn[:, fc, :], wdn_d[fc * P:(fc + 1) * P, :], wdnD, writes=[Buf()])
                        for fc in grp:
                            wdnB[fc].w = ev
                k.barrier()
                k.mark('D')
                if stop == 'D':
                    k.mute = True
                with ExitStack() as pe_:
                    esb = lambda n, s, d: sb(n, s, d, pe_)
                    TT = 256
                    hs = [esb("hs%d" % i, [P, 2, D], F32) for i in range(3)]
                    hsB = [Buf() for _ in range(3)]
                    hsD = [k.dsem() for _ in range(3)]
                    u2b2 = [esb("u2b%d" % i, [P, D], BF16) for i in range(2)]
                    u2bB2 = [Buf() for _ in range(2)]
                    u2T = [esb("u2T%d" % i, [P, KC, TT + 2], BF16) for i in range(2)]
                    u2TB = [Buf() for _ in range(2)]
                    hid = esb("hid", [P, NFC, TT], BF16)
                    hidB = [Buf() for _ in range(NFC)]
                    cbuf = [esb("cbuf%d" % i, [P, TT], F32) for i in range(3)]
                    cbufB = [Buf() for _ in range(3)]
                    ge = [esb("ge%d" % i, [P, TT], BF16) for i in range(3)]
                    geB = [Buf() for _ in range(3)]
                    vb = [esb("vb%d" % i, [P, TT], BF16) for i in range(3)]
                    vbB = [Buf() for _ in range(3)]
                    osb = [esb("osb%d" % i, [P, D], F32) for i in range(2)]
                    osbB = [Buf() for _ in range(2)]
                    osbD = [k.dsem() for _ in range(2)]
                    junkE = esb("junkE", [P, D], BF16)
                    junkEB = Buf()
                    stE = esb("stE", [P, 6 * NT], F32)
                    stEB = [Buf() for _ in range(2 * NT)]
                    h_v = h_d.rearrange("(j s p) d -> j p s d", s=2, p=P)
                    octr = 0
                    fcc = 0
                    def pro_dma(j):
                        s = j % 3
                        k.dma("sp", hs[s][:], h_v[j], hsD[s], reads=[hB[2 * j], hB[2 * j + 1]], writes=[hsB[s]])

                    def pro_chain(j):
                        s = j % 3
                        for sub in range(2):
                            ti = 2 * j + sub
                            k.op("act", "activation", reads=[hsB[s]], writes=[junkEB, stEB[ti]], out=junkE[:], in_=hs[s][:, sub, :],
                                 func=AF.Square, accum_out=stE[:, ti:ti + 1])
                            rstd_chain(stE[:, ti:ti + 1], stE[:, NT + ti:NT + ti + 1], stE[:, 2 * NT + ti:2 * NT + ti + 1], D,
                                       [stEB[ti]], stEB[ti])
                            u2b, u2bB = u2b2[sub], u2bB2[sub]
                            k.op("dve", "tensor_scalar", reads=[hsB[s], stEB[ti]], writes=[u2bB], out=u2b[:], in0=hs[s][:, sub, :],
                                 scalar1=stE[:, 2 * NT + ti:2 * NT + ti + 1], scalar2=None, op0=ALU.mult)

                    def pro_pe(j):
                        s = j % 2
                        if j == 0:
                            k.op("pool", "memset", writes=[u2TB[s]], ap=u2T[s][:, :, 0:2], constant=0.0)
                        else:
                            k.op("pool", "tensor_copy", reads=[u2TB[1 - s]], writes=[u2TB[s]], out=u2T[s][:, :, 0:2],
                                 in_=u2T[1 - s][:, :, TT:TT + 2])
                        for sub in range(2):
                            u2b, u2bB = u2b2[sub], u2bB2[sub]
                            for kc in range(KC):
                                bkk = 6 + kc // 4
                                k.op("pe", "matmul", reads=[u2bB, cB], writes=[psB[bkk]], signal=(kc % 4 == 3),
                                     out=ps[bkk][:, (kc % 4) * P:(kc % 4 + 1) * P], lhsT=u2b[:, kc * P:(kc + 1) * P],
                                     rhs=ident[:], start=True, stop=True, skip_group_check=True)
                            k.op("act", "activation", reads=[psB[6]], writes=[u2TB[s]],
                                 out=u2T[s][:, 0:4, 2 + sub * P:2 + (sub + 1) * P],
                                 in_=ps[6][:].rearrange("p (c t) -> p c t", t=P), func=AF.Copy)
                            k.op("dve", "tensor_copy", reads=[psB[7]], writes=[u2TB[s]],
                                 out=u2T[s][:, 4:8, 2 + sub * P:2 + (sub + 1) * P],
                                 in_=ps[7][:].rearrange("p (c t) -> p c t", t=P))

                    pro_dma(0)
                    for j in range(T // TT):
                        s = j % 2
                        if j == 0:
                            pro_dma(1)
                            pro_chain(0)
                            pro_pe(0)
                        if j + 2 < T // TT:
                            pro_dma(j + 2)
                        for fc in range(NFC):
                            pr = fcc % 3
                            fcc += 1
                            bg, bv_ = ((0, 1), (2, 3), (6, 7))[pr]
                            for kc in range(KC):
                                k.op("pe", "matmul", reads=[wupB[kc], u2TB[s]], writes=[psB[bg]], signal=(kc == KC - 1),
                                     out=ps[bg][:, 0:TT + 2], lhsT=wup[:, kc, fc * P:(fc + 1) * P], rhs=u2T[s][:, kc, 0:TT + 2],
                                     start=(kc == 0), stop=(kc == KC - 1))
                            for kc in range(KC):
                                k.op("pe", "matmul", reads=[wupB[kc], u2TB[s]], writes=[psB[bv_]], signal=(kc == KC - 1),
                                     out=ps[bv_][:, 0:TT], lhsT=wup[:, kc, DFF + fc * P:DFF + (fc + 1) * P],
                                     rhs=u2T[s][:, kc, 2:TT + 2], start=(kc == 0), stop=(kc == KC - 1))
                            cw = lambda jj: pv[:, C_CW + fc * 3 + jj:C_CW + fc * 3 + jj + 1]
                            k.op("dve", "tensor_scalar", reads=[psB[bg], pvB], writes=[cbufB[pr]], out=cbuf[pr][:],
                                 in0=ps[bg][:, 2:TT + 2], scalar1=cw(2), scalar2=pv[:, C_CB + fc:C_CB + fc + 1], op0=ALU.mult,
                                 op1=ALU.add)
                            k.op("dve", "scalar_tensor_tensor", reads=[psB[bg], pvB], writes=[cbufB[pr]], out=cbuf[pr][:],
                                 in0=ps[bg][:, 1:TT + 1], scalar=cw(1), in1=cbuf[pr][:], op0=ALU.mult, op1=ALU.add)
                            k.op("dve", "scalar_tensor_tensor", reads=[psB[bg], pvB], writes=[cbufB[pr]], out=cbuf[pr][:],
                                 in0=ps[bg][:, 0:TT], scalar=cw(0), in1=cbuf[pr][:], op0=ALU.mult, op1=ALU.add)
                            k.op("act", "activation", reads=[psB[bv_]], writes=[vbB[pr]], out=vb[pr][:], in_=ps[bv_][:, 0:TT], func=AF.Copy)
                            k.op("act", "activation", reads=[cbufB[pr]], writes=[geB[pr]], out=ge[pr][:], in_=cbuf[pr][:], func=AF.Gelu)
                            k.op("pool", "tensor_tensor", reads=[geB[pr], vbB[pr]], writes=[hidB[fc]], out=hid[:, fc, :], in0=vb[pr][:],
                                 in1=ge[pr][:], op=ALU.mult)
                        if j + 1 < T // TT:
                            pro_chain(j + 1)
                        epi = []
                        for sub in range(2):
                            ti = 2 * j + sub
                            o = octr % 2
                            octr += 1
                            for half in range(2):
                                bk = 4 + half
                                for fc in range(NFC):
                                    k.op("pe", "matmul", reads=[hidB[fc], wdnB[fc]], writes=[psB[bk]], signal=(fc == NFC - 1),
                                         out=ps[bk][:], lhsT=hid[:, fc, sub * P:(sub + 1) * P], rhs=wdn[:, fc, half * 512:(half + 1) * 512],
                                         start=(fc == 0), stop=(fc == NFC - 1))
                                k.op("dve", "tensor_tensor", reads=[psB[bk], hsB[j % 3]], writes=[osbB[o]],
                                     out=osb[o][:, half * 512:(half + 1) * 512], in0=ps[bk][:],
                                     in1=hs[j % 3][:, sub, half * 512:(half + 1) * 512], op=ALU.add)
                            k.op("act", "activation", reads=[osbB[o]], writes=[junkEB, stEB[NT + ti]], out=junkE[:], in_=osb[o][:],
                                 func=AF.Square, accum_out=stE[:, 3 * NT + ti:3 * NT + ti + 1])
                            rstd_chain(stE[:, 3 * NT + ti:3 * NT + ti + 1], stE[:, 4 * NT + ti:4 * NT + ti + 1],
                                       stE[:, 5 * NT + ti:5 * NT + ti + 1], D, [stEB[NT + ti]], stEB[NT + ti])
                            epi.append((ti, o))
                        if j + 1 < T // TT:
                            pro_pe(j + 1)
                        for ti, o in epi:
                            k.op("dve", "scalar_tensor_tensor", reads=[osbB[o], stEB[NT + ti], gFB], writes=[osbB[o]], out=osb[o][:],
                                 in0=osb[o][:], scalar=stE[:, 5 * NT + ti:5 * NT + ti + 1], in1=gF[:], op0=ALU.mult, op1=ALU.mult)
                            k.dma("pool", out_d[ti * P:(ti + 1) * P, :], osb[o][:], osbD[o], reads=[osbB[o]], is_out=True)
                    k.barrier()

        except Stop:
            pass
        k.mute = False
        k.barrier()

        with nc.Block() as block:
            @block.sync
            def _(h):
                k.replay("sp", h)

            @block.tensor
            def _(h):
                k.replay("pe", h)

            @block.scalar
            def _(h):
                k.replay("act", h)

            @block.vector
            def _(h):
                k.replay("dve", h)

            @block.gpsimd
            def _(h):
                k.replay("pool", h)
    build.marks = k.marks
    return nc


def _col(v):
    v = np.asarray(v, np.float32).reshape(-1, P)
    return np.ascontiguousarray(v.T)


def _prep(inputs):
    f = lambda a: np.ascontiguousarray(np.asarray(a, np.float32))
    pvec = np.zeros((P, 128), np.float32)
    pvec[:, 0:8] = _col(inputs["norm1_g"][0])
    pvec[:, 8:16] = _col(np.concatenate([np.asarray(inputs["attn_norm_g"][0]), np.asarray(inputs["hgrn_norm_g"][0])]))
    pvec[:, 16:24] = _col(inputs["norm2_g"][0])
    lbl = np.asarray(inputs["hgrn_lb_logits"], np.float32)
    pvec[:, 24:28] = _col(lbl[0])
    pvec[:, 28:32] = _col(lbl[1])
    cw = np.asarray(inputs["conv_w"][0], np.float32)
    for j in range(3):
        pvec[:, 32 + j:32 + 66:3] = _col(cw[j])
    pvec[:, 98:120] = _col(inputs["conv_b"][0])
    shared = {"w_in": f(inputs["w_in"][0]), "w_out": f(inputs["w_out"][0]), "w_up": f(inputs["w_up"][0]),
              "w_down": f(inputs["w_down"][0]), "pvec": pvec, "gfin": f(inputs["final_norm_g"])}
    x = f(inputs["x"])
    return [dict(shared, x=x[i]) for i in range(NCORES)]


def kernel(**inputs):
    nc = build()
    in_maps = _prep(inputs)
    res = run_bass_kernel_spmd(nc, in_maps, core_ids=list(range(NCORES)))
    return np.stack([np.asarray(r["out"], np.float32) for r in res.results], axis=0)
```

```python
import numpy as np
from contextlib import ExitStack
import concourse.bass as bass
import concourse.mybir as mybir
from concourse.bass_utils import run_bass_kernel_spmd

F32 = mybir.dt.float32
BF16 = mybir.dt.bfloat16
AF = mybir.ActivationFunctionType
ALU = mybir.AluOpType

P = 128
T = 4096
D = 1024
KC = 8
NT = T // P
DFF = 2816
NFC = DFF // P
EPS = 1e-6
NEG = -30000.0
NCORES = 8


class Ev:
    __slots__ = ("sem", "val", "pe")

    def __init__(self, sem, val, pe=False):
        self.sem, self.val, self.pe = sem, val, pe


class Buf:
    __slots__ = ("name", "w", "r")

    def __init__(self, name=""):
        self.name, self.w, self.r = name, None, {}


class Eng:
    def __init__(self, name, sem, is_pe=False):
        self.name, self.sem, self.is_pe = name, sem, is_pe
        self.cnt = 0
        self.ops = []
        self.waited = {}
        self.pending = []


class DSem:
    def __init__(self, sem):
        self.sem, self.cnt = sem, 0
        self.nobarrier = False


class K:
    def __init__(self, nc, es):
        self.nc, self.es = nc, es
        self.eng = {}
        for n, pe in (("pe", True), ("act", False), ("dve", False), ("pool", False), ("sp", False)):
            self.eng[n] = Eng(n, es.enter_context(nc.semaphore("sem_" + n)), pe)
        self.nds = 0
        self.out_evs = []
        self.dsems = []
        self.mute = False
        self.marks = []

    def dsem(self):
        self.nds += 1
        d = DSem(self.es.enter_context(self.nc.semaphore("dsem%d" % self.nds)))
        self.dsems.append(d)
        return d

    def _wait(self, e, ev):
        if ev is None:
            return
        if e.is_pe and ev.pe:
            return
        assert ev.val is not None, "unresolved event"
        k = id(ev.sem)
        if e.waited.get(k, 0) >= ev.val:
            return
        e.waited[k] = ev.val
        e.ops.append(("w", ev.sem, ev.val))

    def _deps(self, e, reads, writes):
        for b in reads:
            self._wait(e, b.w)
        for b in writes:
            self._wait(e, b.w)
            for ev in b.r.values():
                self._wait(e, ev)

    def _mark(self, ev, reads, writes):
        for b in reads:
            b.r[id(ev.sem)] = ev
        for b in writes:
            b.w = ev
            b.r = {}

    def op(self, en, method, reads=(), writes=(), signal=True, **kw):
        if self.mute:
            return None
        e = self.eng[en]
        self._deps(e, reads, writes)
        ev = Ev(e.sem, None, e.is_pe)
        if signal:
            e.cnt += 1
            ev.val = e.cnt
            for p in e.pending:
                p.val = e.cnt
            e.pending = []
        else:
            assert e.is_pe
            e.pending.append(ev)
        e.ops.append(("op", method, kw, signal))
        self._mark(ev, reads, writes)
        return ev

    def dma(self, en, out, in_, ds, reads=(), writes=(), is_out=False):
        if self.mute:
            return None
        e = self.eng[en]
        self._deps(e, reads, writes)
        ds.cnt += 16
        ev = Ev(ds.sem, ds.cnt)
        e.ops.append(("dma", out, in_, ds.sem))
        self._mark(ev, reads, writes)
        if is_out:
            self.out_evs.append(ev)
        return ev

    def mark(self, name):
        self.marks.append((name, {n: sum(1 for o in e.ops if o[0] == 'op') for n, e in self.eng.items()}))

    def barrier(self):
        if self.mute:
            return
        names = ("pe", "act", "dve", "pool")
        pe = self.eng["pe"]
        if pe.pending:
            pe.cnt += 1
            for p in pe.pending:
                p.val = pe.cnt
            pe.pending = []
            pe.ops.append(("sig",))
        for en in names + ("sp",):
            e = self.eng[en]
            for on in names:
                o = self.eng[on]
                if o is e or o.cnt == 0:
                    continue
                if e.waited.get(id(o.sem), 0) < o.cnt:
                    e.waited[id(o.sem)] = o.cnt
                    e.ops.append(("w", o.sem, o.cnt))
            for ds in self.dsems:
                if ds.nobarrier:
                    continue
                if ds.cnt and e.waited.get(id(ds.sem), 0) < ds.cnt:
                    e.waited[id(ds.sem)] = ds.cnt
                    e.ops.append(("w", ds.sem, ds.cnt))

    def replay(self, en, h):
        e = self.eng[en]
        for o in e.ops:
            if o[0] == "w":
                h.wait_ge(o[1], o[2])
            elif o[0] == "sig":
                h.drain().then_inc(e.sem, 1)
            elif o[0] == "op":
                ins = getattr(h, o[1])(**o[2])
                if o[3]:
                    ins.then_inc(e.sem, 1)
            else:
                h.dma_start(out=o[1], in_=o[2]).then_inc(o[3], 16)


class Stop(Exception):
    pass


def build(debug=False, stop=None):
    nc = bass.Bass("TRN2", target_bir_lowering=False)
    x_d = nc.dram_tensor("x", [T, D], F32, kind="ExternalInput").ap()
    win_d = nc.dram_tensor("w_in", [D, 3584], F32, kind="ExternalInput").ap()
    wout_d = nc.dram_tensor("w_out", [D, D], F32, kind="ExternalInput").ap()
    wup_d = nc.dram_tensor("w_up", [D, 2 * DFF], F32, kind="ExternalInput").ap()
    wdn_d = nc.dram_tensor("w_down", [DFF, D], F32, kind="ExternalInput").ap()
    pv_d = nc.dram_tensor("pvec", [P, 128], F32, kind="ExternalInput").ap()
    gf_d = nc.dram_tensor("gfin", [D], F32, kind="ExternalInput").ap()
    out_d = nc.dram_tensor("out", [T, D], F32, kind="ExternalOutput").ap()
    mixed_d = nc.dram_tensor("mixed_scr", [D, T], BF16, kind="ExternalOutput" if debug else "Internal").ap()
    h_d = nc.dram_tensor("h_scr", [T, D], F32, kind="ExternalOutput" if debug else "Internal").ap()
    C_G1, C_GMIX, C_G2, C_LBL, C_CW, C_CB = 0, 8, 16, 24, 32, 98

    with ExitStack() as es:
        k = K(nc, es)
        sb = lambda n, s, d, st=es: st.enter_context(nc.sbuf_tensor(n, s, d))
        ps = [es.enter_context(nc.psum_tensor("ps%d" % i, [P, 512], F32)) for i in range(8)]
        psB = [Buf("ps%d" % i) for i in range(8)]
        ident = sb("ident", [P, P], BF16)
        identf = sb("identf", [P, P], F32)
        zf = sb("zf", [P, 256], F32)
        hmaskf = sb("hmaskf", [P, P], F32)
        hmask = sb("hmask", [P, P], BF16)
        ones = sb("ones", [P, P], BF16)
        maskp = sb("maskp", [P, 2, P], BF16)
        resetm = sb("resetm", [P, 1024], F32)
        pv = sb("pv", [P, 128], F32)
        lbt = sb("lbt", [P, 16], F32)
        nhalf = sb("nhalf", [P, 1], F32)
        gF = sb("gF", [P, D], F32)
        cB = Buf("consts")
        mixB = [Buf("mix%d" % i) for i in range(8)]
        pvB = Buf("pv")
        gFB = Buf("gF")

        k.dma("sp", pv[:], pv_d[:, :], k.dsem(), writes=[pvB])
        k.dma("sp", gF[:], gf_d.partition_broadcast(P), k.dsem(), writes=[gFB])
        k.op("pool", "memset", writes=[cB], ap=identf[:], constant=1.0)
        k.op("pool", "affine_select", reads=[cB], writes=[cB], out=identf[:], in_=identf[:], pattern=[[-1, P]],
             compare_op=ALU.is_equal, fill=0.0, base=0, channel_multiplier=1)
        k.op("pool", "tensor_copy", reads=[cB], writes=[cB], out=ident[:], in_=identf[:])
        k.op("pool", "memset", reads=[cB], writes=[cB], ap=zf[:], constant=1.0)
        k.op("pool", "memset", reads=[cB], writes=[cB], ap=hmaskf[:], constant=1.0)
        k.op("pool", "affine_select", reads=[cB], writes=[cB], out=hmaskf[:], in_=hmaskf[:], pattern=[[1, P]],
             compare_op=ALU.is_ge, fill=0.0, base=0, channel_multiplier=-1)
        k.op("pool", "memset", reads=[cB], writes=[cB], ap=hmaskf[0:64, 64:128], constant=0.0)
        k.op("pool", "tensor_copy", reads=[cB], writes=[cB], out=hmask[:], in_=hmaskf[:])
        k.op("pool", "memset", reads=[cB], writes=[cB], ap=ones[:], constant=1.0)
        k.op("pool", "affine_select", reads=[cB], writes=[cB], out=maskp[:], in_=zf[:].rearrange("p (h q) -> p h q", h=2),
             pattern=[[0, 2], [-1, P]],
             compare_op=ALU.is_ge, fill=0.0, base=0, channel_multiplier=1)
        k.op("pool", "memset", reads=[cB], writes=[cB], ap=resetm[:], constant=1.0)
        k.op("pool", "memset", reads=[cB], writes=[cB],
             ap=resetm[:].rearrange("p (c k) -> p c k", k=64)[:, :, 0:1], constant=0.0)
        k.op("pool", "memset", reads=[cB], writes=[cB], ap=nhalf[:], constant=-0.5)
        k.op("dve", "tensor_tensor", reads=[pvB], writes=[cB], out=lbt[:, 8:12], in0=pv[:, C_LBL:C_LBL + 4],
             in1=pv[:, C_LBL + 4:C_LBL + 8], op=ALU.subtract)
        k.op("act", "activation", reads=[cB], writes=[cB], out=lbt[:, 0:4], in_=lbt[:, 8:12], func=AF.Sigmoid)
        k.op("dve", "tensor_scalar", reads=[cB], writes=[cB], out=lbt[:, 4:8], in0=lbt[:, 0:4], scalar1=-1.0,
             scalar2=1.0, op0=ALU.mult, op1=ALU.add)

        def rstd_chain(ss_ap, ssn_ap, rstd_ap, n, bufs_r, buf_w, src_psum=False):
            k.op("dve", "tensor_scalar", reads=bufs_r, writes=[buf_w], out=ssn_ap, in0=ss_ap, scalar1=1.0 / n,
                 scalar2=EPS, op0=ALU.mult, op1=ALU.add)
            k.op("pool", "tensor_tensor", reads=[buf_w, cB], writes=[buf_w], out=rstd_ap, in0=ssn_ap,
                 in1=nhalf[:, 0:1], op=ALU.pow)

        try:
            with ExitStack() as ms:
                msb = lambda n, s, d: sb(n, s, d, ms)
                xT = msb("xT", [P, KC, T], BF16)
                xTB = [Buf("xT%d" % i) for i in range(NT)]
                xTB2 = [Buf("xTb%d" % i) for i in range(NT)]
                wst = [msb("wst%d" % i, [P, KC, P], F32) for i in range(2)]
                wstB = [Buf() for _ in range(2)]
                wstD = [k.dsem() for _ in range(2)]
                wbf = [[msb("wbf%d_%d" % (g, j), [P, KC, P], BF16) for j in range(4)] for g in range(2)]
                wbfB = [[Buf() for _ in range(4)] for _ in range(2)]
                win_v = win_d.rearrange("(kc p) n -> p kc n", p=P)
                wctr = [0]

                def load_wblock(gslot, j, col0):
                    s = wctr[0] % 2
                    wctr[0] += 1
                    k.dma("sp", wst[s][:], win_v[:, :, col0:col0 + P], wstD[s], writes=[wstB[s]])
                    for kc in range(KC):
                        k.op("pool", "tensor_scalar", reads=[wstB[s], pvB], writes=[wbfB[gslot][j]],
                             out=wbf[gslot][j][:, kc, :], in0=wst[s][:, kc, :], scalar1=pv[:, C_G1 + kc:C_G1 + kc + 1],
                             scalar2=1.0, op0=ALU.mult, op1=ALU.mult)

                groups = []
                for hp in range(4):
                    groups.append(("attn", hp, [128 * hp, 512 + 128 * hp, 1024 + 128 * hp]))
                for hh in range(4):
                    groups.append(("hgrn", hh, [1536 + 128 * hh, 2048 + 128 * hh, 2560 + 128 * hh, 3072 + 128 * hh]))

                def load_group(gi):
                    if gi >= len(groups):
                        return
                    for j, c0 in enumerate(groups[gi][2]):
                        load_wblock(gi % 2, j, c0)

                load_group(0)

                with ExitStack() as pa:
                    NXS = 8
                    xin = [sb("xin%d" % i, [P, D], F32, pa) for i in range(NXS)]
                    xinB = [Buf() for _ in range(NXS)]
                    xinD = [k.dsem() for _ in range(NXS)]
                    junk = [sb("junk%d" % i, [P, D], BF16, pa) for i in range(2)]
                    junkB = [Buf() for _ in range(2)]
                    xb = [sb("xb%d" % i, [P, D], BF16, pa) for i in range(2)]
                    xbB = [Buf() for _ in range(2)]
                    st = sb("stA", [P, 3 * NT], F32, pa)
                    stB = [Buf() for _ in range(NT)]
                    for i in range(NT):
                        s = i % NXS
                        k.dma("sp", xin[s][:], x_d[i * P:(i + 1) * P, :], xinD[s], writes=[xinB[s]])
                        k.op("act", "activation", reads=[xinB[s]], writes=[junkB[i % 2], stB[i]], out=junk[i % 2][:],
                             in_=xin[s][:], func=AF.Square, accum_out=st[:, i:i + 1])
                        rstd_chain(st[:, i:i + 1], st[:, NT + i:NT + i + 1], st[:, 2 * NT + i:2 * NT + i + 1], D,
                                   [stB[i]], stB[i])
                        k.op("dve", "tensor_scalar", reads=[xinB[s], stB[i]], writes=[xbB[i % 2]], out=xb[i % 2][:],
                             in0=xin[s][:], scalar1=st[:, 2 * NT + i:2 * NT + i + 1], scalar2=None, op0=ALU.mult)
                        bp = (i % 4) * 2
                        for kc in range(KC):
                            bkk = bp + kc // 4
                            k.op("pe", "matmul", reads=[xbB[i % 2], cB], writes=[psB[bkk]], signal=(kc % 4 == 3),
                                 out=ps[bkk][:, (kc % 4) * P:(kc % 4 + 1) * P], lhsT=xb[i % 2][:, kc * P:(kc + 1) * P],
                                 rhs=ident[:], start=True, stop=True, skip_group_check=True)
                        k.op("act", "activation", reads=[psB[bp]], writes=[xTB[i]], out=xT[:, 0:4, i * P:(i + 1) * P],
                             in_=ps[bp][:].rearrange("p (c t) -> p c t", t=P), func=AF.Copy)
                        k.op("dve", "tensor_copy", reads=[psB[bp + 1]], writes=[xTB2[i]], out=xT[:, 4:8, i * P:(i + 1) * P],
                             in_=ps[bp + 1][:].rearrange("p (c t) -> p c t", t=P))
                k.barrier()
                k.mark('A')
                if stop == 'A':
                    k.mute = True
                def proj_fm(wt, wB, c, bank, tok0=None, ntok=512):
                    t0 = c * 512 if tok0 is None else tok0
                    tiles = sorted(set(range(t0 // P, (t0 + ntok - 1) // P + 1)))
                    for kc in range(KC):
                        k.op("pe", "matmul", reads=[wB] + [xTB[t] for t in tiles] + [xTB2[t] for t in tiles], writes=[psB[bank]],
                             signal=(kc == KC - 1), out=ps[bank][:, 0:ntok], lhsT=wt[:, kc, :],
                             rhs=xT[:, kc, t0:t0 + ntok], start=(kc == 0), stop=(kc == KC - 1))

                with ExitStack() as pb:
                    bsb = lambda n, s, d: sb(n, s, d, pb)
                    qT2 = bsb("qT2", [P, 2, T], BF16)
                    qzB = Buf("qzero")
                    k.op("pool", "memset", writes=[qzB], ap=qT2[64:128, 0, :], constant=0.0)
                    k.op("pool", "memset", writes=[qzB], ap=qT2[0:64, 1, :], constant=0.0)
                    kT = bsb("kT", [P, T], BF16)
                    VT = bsb("VT", [P, T], BF16)
                    VTB = [Buf() for _ in range(8)]
                    qB = [Buf() for _ in range(8)]
                    kB = [Buf() for _ in range(8)]
                    DILS = (1, 4, 16)
                    V = [bsb("V%d" % di, [P, NT, P], BF16) for di in range(3)]
                    VB = [[Buf() for _ in range(NT)] for _ in range(3)]
                    pT = [bsb("pT%d" % i, [P, 2, 256], BF16) for i in range(4)]
                    pTB = [Buf() for _ in range(4)]
                    pTBp = [Buf() for _ in range(4)]
                    Oacc = bsb("Oacc", [P, T], F32)
                    Lacc = bsb("Lacc", [P, T], F32)
                    OB = [Buf() for _ in range(8)]
                    LB_ = [Buf() for _ in range(8)]
                    rl = [bsb("rl%d" % i, [P, 512], F32) for i in range(2)]
                    rlB = [Buf() for _ in range(2)]
                    at = [bsb("at%d" % i, [P, 512], BF16) for i in range(2)]
                    atB = [Buf() for _ in range(2)]
                    atD = [k.dsem() for _ in range(2)]
                    SB_BANKS = (0, 1, 6)
                    O_BANKS = ((2, 3), (4, 5))
                    PJ_BANKS = (0, 1)
                    sctr = [0]
                    pjc = [0]
                    for gi in range(4):
                        hp = groups[gi][1]
                        gs = gi % 2
                        load_group(gi + 1)
                        wq, wk, wv = wbf[gs][0], wbf[gs][1], wbf[gs][2]
                        for c in range(8):
                            for which in ("q", "k"):
                                bank = PJ_BANKS[pjc[0] % 2]
                                pjc[0] += 1
                                if which == "q":
                                    proj_fm(wq, wbfB[gs][0], c, bank)
                                    k.op("act", "activation", reads=[psB[bank]], writes=[qB[c]],
                                         out=qT2[0:64, 0, c * 512:(c + 1) * 512], in_=ps[bank][0:64, :], func=AF.Copy, scale=0.125)
                                    k.op("dve", "tensor_scalar", reads=[psB[bank]], writes=[qB[c]],
                                         out=qT2[64:128, 1, c * 512:(c + 1) * 512], in0=ps[bank][64:128, :], scalar1=0.125,
                                         scalar2=None, op0=ALU.mult)
                                else:
                                    proj_fm(wk, wbfB[gs][1], c, bank)
                                    if c % 2:
                                        k.op("act", "activation", reads=[psB[bank]], writes=[kB[c]],
                                             out=kT[:, c * 512:(c + 1) * 512], in_=ps[bank][:], func=AF.Copy)
                                    else:
                                        k.op("dve", "tensor_copy", reads=[psB[bank]], writes=[kB[c]],
                                             out=kT[:, c * 512:(c + 1) * 512], in_=ps[bank][:])
                        for c in range(8):
                            bank = PJ_BANKS[pjc[0] % 2]
                            pjc[0] += 1
                            proj_fm(wv, wbfB[gs][2], c, bank)
                            if c % 2:
                                k.op("act", "activation", reads=[psB[bank]], writes=[VTB[c]],
                                     out=VT[:, c * 512:(c + 1) * 512], in_=ps[bank][:], func=AF.Copy)
                            else:
                                k.op("dve", "tensor_copy", reads=[psB[bank]], writes=[VTB[c]],
                                     out=VT[:, c * 512:(c + 1) * 512], in_=ps[bank][:])
                        for di, d in enumerate(DILS):
                            nb = NT // d
                            for b0 in range(0, NT, 4):
                                bank = PJ_BANKS[pjc[0] % 2]
                                pjc[0] += 1
                                for j in range(4):
                                    b = b0 + j
                                    r, n = b // nb, b % nb
                                    t0 = n * P * d
                                    chs = sorted(set(t // 4 for t in range(t0 // P, t0 // P + d)))
                                    lt = VT[:, t0:t0 + P * d].rearrange("p (i s) -> p i s", s=d)[:, :, r]
                                    k.op("pe", "matmul", reads=[cB] + [VTB[c] for c in chs], writes=[psB[bank]], signal=(j == 3),
                                         out=ps[bank][:, j * P:(j + 1) * P], lhsT=lt, rhs=ident[:], start=True, stop=True,
                                         skip_group_check=True)
                                eng = "act" if pjc[0] % 2 else "dve"
                                kw = {"func": AF.Copy} if eng == "act" else {}
                                k.op(eng, "activation" if eng == "act" else "tensor_copy", reads=[psB[bank]],
                                     writes=[VB[di][b0 + j] for j in range(4)], out=V[di][:, b0:b0 + 4, :],
                                     in_=ps[bank][:].rearrange("p (j f) -> p j f", f=P), **kw)
                        fresh = [True, True]

                        def stage1(di, d, b):
                            nb = NT // d
                            r, n = b // nb, b % nb
                            has_next = (n + 1 < nb)
                            nq = 256 if has_next else 128
                            t0 = n * P * d
                            sbk = SB_BANKS[sctr[0] % 3]
                            pslot = sctr[0] % 4
                            sctr[0] += 1
                            sview = ps[sbk][:, 0:2 * nq].rearrange("p (h q) -> p h q", h=2)
                            ktiles = list(range(t0 // P, t0 // P + d))
                            qtiles = list(range(t0 // P, min(NT, t0 // P + (2 if has_next else 1) * d)))
                            kchunks = sorted(set(t // 4 for t in ktiles))
                            qchunks = sorted(set(t // 4 for t in qtiles))
                            lt = kT[:, t0:t0 + P * d].rearrange("p (i s) -> p i s", s=d)[:, :, r]
                            rt = qT2[:, :, t0:t0 + nq * d].rearrange("p h (i s) -> p h i s", s=d)[:, :, :, r]
                            k.op("pe", "matmul", reads=[kB[c] for c in kchunks] + [qB[c] for c in qchunks] + [qzB],
                                 writes=[psB[sbk]], out=sview, lhsT=lt, rhs=rt, start=True, stop=True)
                            k.op("act", "activation", reads=[psB[sbk]], writes=[pTB[pslot]] + ([pTBp[pslot]] if has_next else []),
                                 out=pT[pslot][:, :, 0:nq], in_=sview, func=AF.Exp)
                            k.op("pool", "affine_select", reads=[pTB[pslot]], writes=[pTB[pslot]], out=pT[pslot][:, :, 0:P],
                                 in_=pT[pslot][:, :, 0:P], pattern=[[0, 2], [1, P]], compare_op=ALU.is_ge, fill=0.0, base=0,
                                 channel_multiplier=-1)
                            if has_next:
                                k.op("dve", "tensor_tensor", reads=[pTBp[pslot], cB], writes=[pTBp[pslot]],
                                     out=pT[pslot][:, :, P:2 * P], in0=pT[pslot][:, :, P:2 * P], in1=maskp[:], op=ALU.mult)
                            return (di, d, b, has_next, pslot)

                        def stage2(item):
                            di, d, b, has_next, pslot = item
                            cch = b // 4
                            par = cch % 2
                            bn, bl = O_BANKS[par]
                            col = (b % 4) * P
                            parts = [(0, P, par, col)]
                            if has_next:
                                if b % 4 == 3:
                                    parts.append((P, P, 1 - par, 0))
                                else:
                                    parts[0] = (0, 256, par, col)
                            for pi, (pc0, wd, pr, oc) in enumerate(parts):
                                bn_, bl_ = O_BANKS[pr]
                                for h in range(2):
                                    hs_ = slice(h * 64, (h + 1) * 64)
                                    first = fresh[pr]
                                    prd = ([pTB[pslot]] if pc0 == 0 else []) + ([pTBp[pslot]] if pc0 + wd > P else [])
                                    k.op("pe", "matmul", reads=[VB[di][b]] + prd, writes=[psB[bn_]], signal=False,
                                         out=ps[bn_][hs_, oc:oc + wd], lhsT=V[di][:, b, hs_],
                                         rhs=pT[pslot][:, h, pc0:pc0 + wd], start=first, stop=True,
                                         skip_group_check=True)
                                    k.op("pe", "matmul", reads=[cB] + prd, writes=[psB[bl_]],
                                         signal=(h == 1), out=ps[bl_][hs_, oc:oc + wd], lhsT=ones[:, 0:64],
                                         rhs=pT[pslot][:, h, pc0:pc0 + wd], start=first, stop=True,
                                         skip_group_check=True)
                                fresh[pr] = False
                            if b % 4 == 3:
                                if d == 1:
                                    do, dl = Oacc[:, cch * 512:(cch + 1) * 512], Lacc[:, cch * 512:(cch + 1) * 512]
                                    so, sl = ps[bn][:], ps[bl][:]
                                    chs = [cch]
                                elif d == 4:
                                    rr, half = cch // 2, cch % 2
                                    v = lambda A: A[:, half * 2048:(half + 1) * 2048].rearrange("p (m s) -> p m s", s=4)[:, :, rr]
                                    do, dl = v(Oacc), v(Lacc)
                                    so, sl = ps[bn][:], ps[bl][:]
                                    chs = list(range(half * 4, half * 4 + 4))
                                else:
                                    v = lambda A: A[:, :].rearrange("p (m s) -> p m s", s=16)[:, :, 2 * cch:2 * cch + 2]
                                    do, dl = v(Oacc), v(Lacc)
                                    so = ps[bn][:].rearrange("p (r m) -> p m r", r=2)
                                    sl = ps[bl][:].rearrange("p (r m) -> p m r", r=2)
                                    chs = list(range(8))
                                if d == 1:
                                    k.op("dve", "tensor_copy", reads=[psB[bn]], writes=[OB[c] for c in chs], out=do, in_=so)
                                    k.op("act", "activation", reads=[psB[bl]], writes=[LB_[c] for c in chs], out=dl, in_=sl,
                                         func=AF.Copy)
                                else:
                                    k.op("dve", "tensor_tensor", reads=[psB[bn]], writes=[OB[c] for c in chs], out=do,
                                         in0=so, in1=do, op=ALU.add)
                                    k.op("dve", "tensor_tensor", reads=[psB[bl]], writes=[LB_[c] for c in chs], out=dl,
                                         in0=sl, in1=dl, op=ALU.add)
                                fresh[par] = True

                        work = [(di, d, b) for di, d in enumerate(DILS) for b in range(NT)]
                        q_items = [stage1(*work[0]), stage1(*work[1])]
                        for wi in range(len(work)):
                            if wi + 2 < len(work):
                                q_items.append(stage1(*work[wi + 2]))
                            stage2(q_items.pop(0))
                        for c in range(8):
                            k.op("act", "activation", reads=[LB_[c]], writes=[LB_[c]], out=Lacc[:, c * 512:(c + 1) * 512],
                                 in_=Lacc[:, c * 512:(c + 1) * 512], func=AF.Ln)
                        for c in range(8):
                            s = c % 2
                            k.op("act", "activation", reads=[LB_[c]], writes=[rlB[s]], out=rl[s][:],
                                 in_=Lacc[:, c * 512:(c + 1) * 512], func=AF.Exp, scale=-1.0)
                            k.op("dve", "tensor_tensor", reads=[OB[c], rlB[s]], writes=[atB[s]], out=at[s][:],
                                 in0=Oacc[:, c * 512:(c + 1) * 512], in1=rl[s][:], op=ALU.mult)
                            k.dma("pool", mixed_d[hp * P:(hp + 1) * P, c * 512:(c + 1) * 512], at[s][:], atD[s],
                                  reads=[atB[s]], writes=[mixB[hp]])
                k.barrier()
                k.mark('B')
                if stop == 'B':
                    k.mute = True
                with ExitStack() as pc:
                    csb = lambda n, s, d: sb(n, s, d, pc)
                    SEG = 1024
                    f32b = {}
                    for nm in ("SG", "SQ", "QF", "S3", "FG", "KY", "LF", "Bc", "ENB", "BD", "EKD", "REC", "LN1", "RS", "R1"):
                        f32b[nm] = (csb("h_" + nm, [P, SEG], F32), Buf(nm))
                    A = lambda nm: f32b[nm][0]
                    Bf = lambda nm: f32b[nm][1]
                    dbl = {}
                    for nm, dt_ in (("QD", BF16), ("KI", BF16), ("KE", BF16), ("EB", F32), ("GT", F32)):
                        dbl[nm] = ([csb("h_%s%d" % (nm, i), [P, SEG], dt_) for i in range(2)], [Buf(nm) for _ in range(2)])
                    RSQ, RSQB = csb("h_RSQ", [P, SEG], BF16), Buf("RSQ")
                    MX = [csb("h_MX%d" % i, [P, SEG], BF16) for i in range(2)]
                    MXB = [Buf() for _ in range(2)]
                    MXD = [k.dsem() for _ in range(2)]
                    IVs = [csb("h_IV%d" % i, [P, 8, P], BF16) for i in range(2)]
                    IVBs = [[Buf() for _ in range(2)] for _ in range(2)]
                    Ke8 = csb("h_Ke8", [P, 8, P], BF16)
                    Ke8B = [Buf() for _ in range(8)]
                    As8 = csb("h_As8", [P, 8, P], BF16)
                    As8B = [Buf() for _ in range(8)]
                    Sf = [csb("h_Sf%d" % i, [P, P], F32) for i in range(3)]
                    SfB = [Buf() for _ in range(3)]
                    S16 = csb("h_S16", [P, 17, P], BF16)
                    S16B = [Buf() for _ in range(17)]
                    mxc = [0]
                    sidx = [0]
                    pjb = [0]

                    def stageA(gi, hh, sg, sl):
                        gs = gi % 2
                        if sg == 0:
                            load_group(gi + 1)
                        wq, wf, wi, wg = wbf[gs]
                        wqB, wfB, wiB, wgB = wbfB[gs]
                        lbc, omlc = lbt[:, hh:hh + 1], lbt[:, 4 + hh:5 + hh]
                        QD, KI, KE, EB, GT = (dbl[n][0][sl] for n in ("QD", "KI", "KE", "EB", "GT"))
                        QDB, KIB, KEB, EBB, GTB = (dbl[n][1][sl] for n in ("QD", "KI", "KE", "EB", "GT"))
                        IV, IVB = IVs[sl], IVBs[sl]

                        def nbank():
                            pjb[0] += 1
                            return (0, 1, 6, 7)[pjb[0] % 4]
                        for cc in range(2):
                            cs = slice(cc * 512, (cc + 1) * 512)
                            bk = nbank()
                            proj_fm(wf, wfB, sg * 2 + cc, bk)
                            k.op("act", "activation", reads=[psB[bk]], writes=[Bf("SG")], out=A("SG")[:, cs], in_=ps[bk][:],
                                 func=AF.Sigmoid)
                            yield
                        for cc in range(2):
                            cs = slice(cc * 512, (cc + 1) * 512)
                            bk = nbank()
                            proj_fm(wq, wqB, sg * 2 + cc, bk)
                            k.op("act", "activation", reads=[psB[bk]], writes=[Bf("QF")], out=A("QF")[:, cs], in_=ps[bk][:],
                                 func=AF.Silu)
                            yield
                        for cc in range(2):
                            cs = slice(cc * 512, (cc + 1) * 512)
                            bk = nbank()
                            proj_fm(wg, wgB, sg * 2 + cc, bk)
                            k.op("act", "activation", reads=[psB[bk]], writes=[GTB], out=GT[:, cs], in_=ps[bk][:],
                                 func=AF.Silu)
                            yield
                        for half in range(2):
                            bank = nbank()
                            for j in range(4):
                                ti = sg * 8 + half * 4 + j
                                for kc in range(KC):
                                    k.op("pe", "matmul", reads=[wiB, xTB[ti], xTB2[ti]], writes=[psB[bank]],
                                         signal=(kc == KC - 1 and j == 3), out=ps[bank][:, j * P:(j + 1) * P],
                                         lhsT=xT[:, kc, ti * P:(ti + 1) * P], rhs=wi[:, kc, :], start=(kc == 0),
                                         stop=(kc == KC - 1))
                            k.op("act", "activation", reads=[psB[bank]], writes=[IVB[half]],
                                 out=IV[:, half * 4:(half + 1) * 4, :], in_=ps[bank][:].rearrange("p (j f) -> p j f", f=P),
                                 func=AF.Copy)
                            yield
                        k.op("dve", "tensor_scalar", reads=[Bf("SG"), cB], writes=[Bf("FG")], out=A("FG")[:], in0=A("SG")[:],
                             scalar1=omlc, scalar2=lbc, op0=ALU.mult, op1=ALU.add)
                        yield
                        k.op("dve", "tensor_scalar", reads=[Bf("FG")], writes=[Bf("KY")], out=A("KY")[:], in0=A("FG")[:],
                             scalar1=-1.0, scalar2=1.0, op0=ALU.mult, op1=ALU.add)
                        yield
                        k.op("act", "activation", reads=[Bf("FG")], writes=[Bf("LF")], out=A("LF")[:], in_=A("FG")[:], func=AF.Ln)
                        yield
                        k.op("dve", "tensor_tensor_scan", reads=[Bf("LF"), cB], writes=[Bf("Bc")], out=A("Bc")[:],
                             data0=resetm[:], data1=A("LF")[:], initial=0.0, op0=ALU.mult, op1=ALU.add)
                        yield
                        k.op("act", "activation", reads=[Bf("Bc")], writes=[EBB], out=EB[:], in_=A("Bc")[:], func=AF.Exp)
                        yield
                        k.op("act", "activation", reads=[Bf("Bc")], writes=[Bf("ENB")], out=A("ENB")[:], in_=A("Bc")[:],
                             func=AF.Exp, scale=-1.0)
                        yield
                        bv = A("Bc")[:].rearrange("p (c k) -> p c k", k=64)
                        k.op("dve", "tensor_tensor", reads=[Bf("Bc")], writes=[Bf("BD")],
                             out=A("BD")[:].rearrange("p (c k) -> p c k", k=64),
                             in0=bv[:, :, 63:64].to_broadcast([P, SEG // 64, 64]), in1=bv, op=ALU.subtract)
                        yield
                        k.op("act", "activation", reads=[Bf("BD")], writes=[Bf("EKD")], out=A("EKD")[:], in_=A("BD")[:], func=AF.Exp)
                        yield
                        k.op("dve", "tensor_tensor", reads=[Bf("QF"), EBB], writes=[QDB], out=QD[:], in0=A("QF")[:],
                             in1=EB[:], op=ALU.mult)
                        yield
                        k.op("pool", "tensor_tensor", reads=[Bf("KY"), Bf("ENB")], writes=[KIB], out=KI[:], in0=A("KY")[:],
                             in1=A("ENB")[:], op=ALU.mult)
                        yield
                        k.op("pool", "tensor_tensor", reads=[Bf("KY"), Bf("EKD")], writes=[KEB], out=KE[:], in0=A("KY")[:],
                             in1=A("EKD")[:], op=ALU.mult)
                        yield

                    def stageB(gi, hh, sg, sl):
                        tk0 = sg * SEG
                        QD, KI, KE, EB, GT = (dbl[n][0][sl] for n in ("QD", "KI", "KE", "EB", "GT"))
                        QDB, KIB, KEB, EBB, GTB = (dbl[n][1][sl] for n in ("QD", "KI", "KE", "EB", "GT"))
                        IV, IVB = IVs[sl], IVBs[sl]
                        if sg == 0:
                            k.op("pool", "memset", writes=[SfB[sidx[0] % 3]], ap=Sf[sidx[0] % 3][:], constant=0.0)
                            k.op("pool", "memset", writes=[S16B[0]], ap=S16[:, 0, :], constant=0.0)
                            first_state = 0
                        else:
                            first_state = None
                        for tl in range(8):
                            ts_ = slice(tl * P, (tl + 1) * P)
                            a = tl % 2
                            k.op("pe", "matmul", reads=[KIB, QDB], writes=[psB[a]], out=ps[a][:, 0:P], lhsT=KI[:, ts_],
                                 rhs=QD[:, ts_], start=True, stop=True, skip_group_check=True)
                            k.op("pe", "matmul", reads=[KEB, cB], writes=[psB[6 + a]], out=ps[6 + a][:, 0:P], lhsT=KE[:, ts_],
                                 rhs=ident[:], start=True, stop=True, skip_group_check=True)
                            k.op("dve", "tensor_tensor", reads=[psB[a], cB], writes=[As8B[tl]], out=As8[:, tl, :], in0=ps[a][:, 0:P],
                                 in1=hmask[:], op=ALU.mult)
                            k.op("act", "activation", reads=[psB[6 + a]], writes=[Ke8B[tl]], out=Ke8[:, tl, :], in_=ps[6 + a][:, 0:P],
                                 func=AF.Copy)
                        for tl in range(8):
                            for c2 in range(2):
                                c = 2 * tl + c2
                                ub = 2 + c % 4
                                k.op("pe", "matmul", reads=[Ke8B[tl], IVB[tl // 4]], writes=[psB[ub]],
                                     out=ps[ub][:, (c // 4) * P:(c // 4 + 1) * P], lhsT=Ke8[c2 * 64:(c2 + 1) * 64, tl, :],
                                     rhs=IV[c2 * 64:(c2 + 1) * 64, tl, :], start=True, stop=True, skip_group_check=True)
                        if sg != 0:
                            k.op("dve", "tensor_copy", reads=[S16B[16]], writes=[S16B[0]], out=S16[:, 0, :], in_=S16[:, 16, :])
                        for c in range(16):
                            cur, nxt = sidx[0] % 3, (sidx[0] + 1) % 3
                            sidx[0] += 1
                            ub = 2 + c % 4
                            ecol = c * 64 + 63
                            k.op("dve", "scalar_tensor_tensor", reads=[SfB[cur], EBB, psB[ub]], writes=[SfB[nxt]],
                                 out=Sf[nxt][:], in0=Sf[cur][:], scalar=EB[:, ecol:ecol + 1],
                                 in1=ps[ub][:, (c // 4) * P:(c // 4 + 1) * P], op0=ALU.mult, op1=ALU.add)
                            k.op("act", "activation", reads=[SfB[nxt]], writes=[S16B[c + 1]], out=S16[:, c + 1, :], in_=Sf[nxt][:],
                                 func=AF.Copy)
                    def stageB2(gi, hh, sg, sl):
                        tk0 = sg * SEG
                        QD, KI, KE, EB, GT = (dbl[n][0][sl] for n in ("QD", "KI", "KE", "EB", "GT"))
                        QDB, KIB, KEB, EBB, GTB = (dbl[n][1][sl] for n in ("QD", "KI", "KE", "EB", "GT"))
                        IV, IVB = IVs[sl], IVBs[sl]
                        for tl in range(8):
                            ob = 6 + tl // 4
                            oc = (tl % 4) * P
                            k.op("pe", "matmul", reads=[IVB[tl // 4], As8B[tl]], writes=[psB[ob]], signal=False,
                                 out=ps[ob][:, oc:oc + P], lhsT=IV[:, tl, :], rhs=As8[:, tl, :], start=(tl % 4 == 0), stop=False,
                                 skip_group_check=True)
                            for c2 in range(2):
                                c = 2 * tl + c2
                                k.op("pe", "matmul", reads=[S16B[c], QDB], writes=[psB[ob]], signal=(c2 == 1),
                                     out=ps[ob][:, oc + c2 * 64:oc + (c2 + 1) * 64], lhsT=S16[:, c, :],
                                     rhs=QD[:, tl * P + c2 * 64:tl * P + (c2 + 1) * 64], start=False, stop=(c2 == 1),
                                     skip_group_check=True)
                        for hf in range(2):
                            k.op("act" if hf == 0 else "dve", "activation" if hf == 0 else "tensor_copy", reads=[psB[6 + hf]],
                                 writes=[Bf("REC")], out=A("REC")[:, hf * 512:(hf + 1) * 512], in_=ps[6 + hf][:],
                                 **({"func": AF.Copy} if hf == 0 else {}))
                        k.op("act", "activation", reads=[Bf("REC")], writes=[RSQB], out=RSQ[:], in_=A("REC")[:], func=AF.Square)
                        for cc in range(2):
                            cs = slice(cc * 512, (cc + 1) * 512)
                            k.op("pe", "matmul", reads=[RSQB, cB], writes=[psB[cc]], out=ps[cc][:], lhsT=ones[:], rhs=RSQ[:, cs],
                                 start=True, stop=True)
                            k.op("act", "activation", reads=[psB[cc]], writes=[Bf("LN1")], out=A("LN1")[:, cs], in_=ps[cc][:],
                                 func=AF.Ln, scale=1.0 / P, bias=EPS)
                        k.op("act", "activation", reads=[Bf("LN1")], writes=[Bf("RS")], out=A("RS")[:], in_=A("LN1")[:], func=AF.Exp,
                             scale=-0.5)
                        k.op("dve", "tensor_tensor", reads=[Bf("REC"), Bf("RS")], writes=[Bf("R1")], out=A("R1")[:], in0=A("REC")[:],
                             in1=A("RS")[:], op=ALU.mult)
                        m = mxc[0] % 2
                        mxc[0] += 1
                        k.op("dve", "tensor_tensor", reads=[Bf("R1"), GTB], writes=[MXB[m]], out=MX[m][:], in0=A("R1")[:],
                             in1=GT[:], op=ALU.mult)
                        k.dma("pool", mixed_d[512 + hh * P:512 + (hh + 1) * P, tk0:tk0 + SEG], MX[m][:], MXD[m], reads=[MXB[m]],
                              writes=[mixB[4 + hh]])

                    hwork = [(gi, groups[gi][1], sg) for gi in range(4, 8) for sg in range(T // SEG)]
                    for _ in stageA(*hwork[0], 0):
                        pass
                    for wi_ in range(len(hwork)):
                        stageB(*hwork[wi_], wi_ % 2)
                        if wi_ + 1 < len(hwork):
                            for _ in stageA(*hwork[wi_ + 1], (wi_ + 1) % 2):
                                pass
                        stageB2(*hwork[wi_], wi_ % 2)
            k.barrier()
            k.mark('C')
            if stop == 'C':
                k.mute = True
            with ExitStack() as fs:
                wup = sb("wup", [P, KC, 2 * DFF], BF16, fs)
                wdn = sb("wdn", [P, NFC, D], BF16, fs)
                wupB = [Buf() for _ in range(KC)]
                wdnB = [Buf() for _ in range(NFC)]
                cast_rr = [0]

                def cast(out, in_, scal, reads, writes):
                    e = ("pool", "act", "dve")[cast_rr[0] % 3]
                    cast_rr[0] += 1
                    if e == "pool":
                        if scal is None:
                            k.op("pool", "tensor_copy", reads=reads, writes=writes, out=out, in_=in_)
                        else:
                            k.op("pool", "tensor_scalar", reads=reads + [pvB], writes=writes, out=out, in0=in_, scalar1=scal,
                                 scalar2=1.0, op0=ALU.mult, op1=ALU.mult)
                    elif e == "act":
                        if scal is None:
                            k.op("act", "activation", reads=reads, writes=writes, out=out, in_=in_, func=AF.Copy)
                        else:
                            k.op("act", "activation", reads=reads + [pvB], writes=writes, out=out, in_=in_, func=AF.Copy,
                                 scale=scal)
                    else:
                        if scal is None:
                            k.op("dve", "tensor_copy", reads=reads, writes=writes, out=out, in_=in_)
                        else:
                            k.op("dve", "tensor_scalar", reads=reads + [pvB], writes=writes, out=out, in0=in_, scalar1=scal,
                                 scalar2=None, op0=ALU.mult)

                hB = [Buf("h%d" % i) for i in range(NT)]
                with ExitStack() as pd:
                    dsb = lambda n, s, d: sb(n, s, d, pd)
                    wo = dsb("wo", [P, KC, D], BF16)
                    woB = [Buf() for _ in range(KC)]
                    wstF = [dsb("wstF%d" % i, [P, 1408], F32) for i in range(2)]
                    wstFB = [Buf() for _ in range(2)]
                    wstFD = [k.dsem() for _ in range(2)]
                    wc = [0]

                    def stage(src_ap, ncols):
                        s = wc[0] % 2
                        wc[0] += 1
                        k.dma("sp", wstF[s][:, 0:ncols], src_ap, wstFD[s], writes=[wstFB[s]])
                        return s

                    for kc in range(KC):
                        s = stage(wout_d[kc * P:(kc + 1) * P, :], D)
                        cast(wo[:, kc, :], wstF[s][:, 0:D], pv[:, C_GMIX + kc:C_GMIX + kc + 1], [wstFB[s]], [woB[kc]])

                    def load_ffn_weights(step):
                        if step < 32:
                            kc, q = step // 4, step % 4
                            s = stage(wup_d[kc * P:(kc + 1) * P, q * 1408:(q + 1) * 1408], 1408)
                            cast(wup[:, kc, q * 1408:(q + 1) * 1408], wstF[s][:, 0:1408], pv[:, C_G2 + kc:C_G2 + kc + 1],
                                 [wstFB[s]], [wupB[kc]])

                    mxt = [dsb("mxt%d" % i, [P, KC, 512], BF16) for i in range(2)]
                    mxtB = [Buf() for _ in range(2)]
                    mxtD = [k.dsem() for _ in range(2)]
                    xin2 = [dsb("xin2_%d" % i, [P, D], F32) for i in range(2)]
                    xin2B = [Buf() for _ in range(2)]
                    xin2D = [k.dsem() for _ in range(2)]
                    hsb = [dsb("hsb%d" % i, [P, D], F32) for i in range(2)]
                    hsbB = [Buf() for _ in range(2)]
                    hsbD = [k.dsem() for _ in range(2)]
                    sq = [dsb("sq%d" % i, [P, 4, P], BF16) for i in range(2)]
                    sqB = [Buf() for _ in range(2)]
                    stD = dsb("stD", [P, 2 * NT], F32)
                    stDB = [Buf() for _ in range(NT)]
                    mixed_v = mixed_d.rearrange("(kc p) t -> p kc t", p=P)
                    wstep = 0
                    for i in range(NT):
                        c, tl = i // 4, i % 4
                        ms_ = c % 2
                        s = i % 2
                        if tl == 0:
                            k.dma("sp", mxt[ms_][:], mixed_v[:, :, c * 512:(c + 1) * 512], mxtD[ms_], reads=mixB,
                                  writes=[mxtB[ms_]])
                        k.dma("sp", xin2[s][:], x_d[i * P:(i + 1) * P, :], xin2D[s], writes=[xin2B[s]])
                        for _ in range(2):
                            load_ffn_weights(wstep)
                            wstep += 1
                        tsl = slice(tl * P, (tl + 1) * P)
                        k.op("pool", "tensor_tensor", reads=[mxtB[ms_]], writes=[sqB[s]], out=sq[s][:], in0=mxt[ms_][:, 0:4, tsl],
                             in1=mxt[ms_][:, 0:4, tsl], op=ALU.mult)
                        sbk = 4 + s
                        for kc in range(4):
                            k.op("pe", "matmul", reads=[sqB[s], cB], writes=[psB[sbk]], signal=(kc == 3), out=ps[sbk][:, 0:2],
                                 lhsT=sq[s][:, kc, :], rhs=ones[:, 0:2], start=(kc == 0), stop=(kc == 3))
                        rstd_chain(ps[sbk][:, 0:1], stD[:, i:i + 1], stD[:, NT + i:NT + i + 1], 512, [psB[sbk]], stDB[i])
                        for half in range(2):
                            ba, br = 2 * half, 2 * half + 1
                            hsl = slice(half * 512, (half + 1) * 512)
                            for kc in range(4):
                                k.op("pe", "matmul", reads=[mxtB[ms_], woB[kc]], writes=[psB[ba]], signal=(kc == 3), out=ps[ba][:],
                                     lhsT=mxt[ms_][:, kc, tsl], rhs=wo[:, kc, hsl], start=(kc == 0), stop=(kc == 3))
                            for kc in range(4, 8):
                                k.op("pe", "matmul", reads=[mxtB[ms_], woB[kc]], writes=[psB[br]], signal=(kc == 7), out=ps[br][:],
                                     lhsT=mxt[ms_][:, kc, tsl], rhs=wo[:, kc, hsl], start=(kc == 4), stop=(kc == 7))
                            k.op("dve", "tensor_tensor", reads=[psB[br], xin2B[s]], writes=[hsbB[s]], out=hsb[s][:, hsl], in0=ps[br][:],
                                 in1=xin2[s][:, hsl], op=ALU.add)
                            k.op("dve", "scalar_tensor_tensor", reads=[psB[ba], stDB[i]], writes=[hsbB[s]], out=hsb[s][:, hsl],
                                 in0=ps[ba][:], scalar=stD[:, NT + i:NT + i + 1], in1=hsb[s][:, hsl], op0=ALU.mult, op1=ALU.add)
                        k.dma("pool", h_d[i * P:(i + 1) * P, :], hsb[s][:], hsbD[s], reads=[hsbB[s]], writes=[hB[i]])
                    while wstep < 32:
                        load_ffn_weights(wstep)
                        wstep += 1
                    for g0 in range(0, NFC, 6):
                        wdnD = k.dsem()
                        wdnD.nobarrier = True
                        grp = list(range(g0, min(NFC, g0 + 6)))
                        for fc in grp:
                            ev = k.dma("pool", wdn[:, fc, :], wdn_d[fc * P:(fc + 1) * P, :], wdnD, writes=[Buf()])
                        for fc in grp:
                            wdnB[fc].w = ev
                k.barrier()
                k.mark('D')
                if stop == 'D':
                    k.mute = True
                with ExitStack() as pe_:
                    esb = lambda n, s, d: sb(n, s, d, pe_)
                    TT = 256
                    hs = [esb("hs%d" % i, [P, 2, D], F32) for i in range(3)]
                    hsB = [Buf() for _ in range(3)]
                    hsD = [k.dsem() for _ in range(3)]
                    u2b2 = [esb("u2b%d" % i, [P, D], BF16) for i in range(2)]
                    u2bB2 = [Buf() for _ in range(2)]
                    u2T = [esb("u2T%d" % i, [P, KC, TT + 2], BF16) for i in range(2)]
                    u2TB = [Buf() for _ in range(2)]
                    u2TQ = [[Buf() for _ in range(4)] for _ in range(2)]
                    hid = esb("hid", [P, NFC, TT], BF16)
                    hidB = [Buf() for _ in range(NFC)]
                    cbuf = [esb("cbuf%d" % i, [P, TT], F32) for i in range(3)]
                    cbufB = [Buf() for _ in range(3)]
                    ge = [esb("ge%d" % i, [P, TT], BF16) for i in range(3)]
                    geB = [Buf() for _ in range(3)]
                    vb = [esb("vb%d" % i, [P, TT], BF16) for i in range(3)]
                    vbB = [Buf() for _ in range(3)]
                    osb = [esb("osb%d" % i, [P, D], F32) for i in range(2)]
                    osbB = [Buf() for _ in range(2)]
                    osbD = [k.dsem() for _ in range(2)]
                    junkE = esb("junkE", [P, D], BF16)
                    junkEB = Buf()
                    stE = esb("stE", [P, 6 * NT], F32)
                    stEB = [Buf() for _ in range(2 * NT)]
                    h_v = h_d.rearrange("(j s p) d -> j p s d", s=2, p=P)
                    octr = 0
                    fcc = 0
                    def pro_dma(j):
                        s = j % 3
                        k.dma("sp", hs[s][:], h_v[j], hsD[s], reads=[hB[2 * j], hB[2 * j + 1]], writes=[hsB[s]])

                    def pro_chain(j):
                        s = j % 3
                        for sub in range(2):
                            ti = 2 * j + sub
                            k.op("act", "activation", reads=[hsB[s]], writes=[junkEB, stEB[ti]], out=junkE[:], in_=hs[s][:, sub, :],
                                 func=AF.Square, accum_out=stE[:, ti:ti + 1])
                            rstd_chain(stE[:, ti:ti + 1], stE[:, NT + ti:NT + ti + 1], stE[:, 2 * NT + ti:2 * NT + ti + 1], D,
                                       [stEB[ti]], stEB[ti])
                            u2b, u2bB = u2b2[sub], u2bB2[sub]
                            k.op("dve", "tensor_scalar", reads=[hsB[s], stEB[ti]], writes=[u2bB], out=u2b[:], in0=hs[s][:, sub, :],
                                 scalar1=stE[:, 2 * NT + ti:2 * NT + ti + 1], scalar2=None, op0=ALU.mult)

                    def pro_pe(j):
                        s = j % 2
                        if j == 0:
                            k.op("pool", "memset", writes=[u2TB[s]], ap=u2T[s][:, :, 0:2], constant=0.0)
                        else:
                            k.op("pool", "tensor_copy", reads=[u2TQ[1 - s][2], u2TQ[1 - s][3]], writes=[u2TB[s]], out=u2T[s][:, :, 0:2],
                                 in_=u2T[1 - s][:, :, TT:TT + 2])
                        for sub in range(2):
                            u2b, u2bB = u2b2[sub], u2bB2[sub]
                            b0_ = 6 if sub == 0 else 4
                            for kc in range(KC):
                                bkk = b0_ + kc // 4
                                k.op("pe", "matmul", reads=[u2bB, cB], writes=[psB[bkk]], signal=(kc % 4 == 3),
                                     out=ps[bkk][:, (kc % 4) * P:(kc % 4 + 1) * P], lhsT=u2b[:, kc * P:(kc + 1) * P],
                                     rhs=ident[:], start=True, stop=True, skip_group_check=True)
                            k.op("act", "activation", reads=[psB[b0_]], writes=[u2TQ[s][sub * 2]],
                                 out=u2T[s][:, 0:4, 2 + sub * P:2 + (sub + 1) * P],
                                 in_=ps[b0_][:].rearrange("p (c t) -> p c t", t=P), func=AF.Copy)
                            k.op("dve", "tensor_copy", reads=[psB[b0_ + 1]], writes=[u2TQ[s][sub * 2 + 1]],
                                 out=u2T[s][:, 4:8, 2 + sub * P:2 + (sub + 1) * P],
                                 in_=ps[b0_ + 1][:].rearrange("p (c t) -> p c t", t=P))

                    pro_dma(0)
                    for j in range(T // TT):
                        s = j % 2
                        if j == 0:
                            pro_dma(1)
                            pro_chain(0)
                            pro_pe(0)
                        if j + 2 < T // TT:
                            pro_dma(j + 2)
                        for fc in range(NFC):
                            pr = fcc % 3
                            fcc += 1
                            bg, bv_ = ((0, 1), (2, 3), (6, 7))[pr]
                            for kc in range(KC):
                                k.op("pe", "matmul", reads=[wupB[kc], u2TB[s]] + u2TQ[s], writes=[psB[bg]], signal=(kc == KC - 1),
                                     out=ps[bg][:, 0:TT + 2], lhsT=wup[:, kc, fc * P:(fc + 1) * P], rhs=u2T[s][:, kc, 0:TT + 2],
                                     start=(kc == 0), stop=(kc == KC - 1))
                            for kc in range(KC):
                                k.op("pe", "matmul", reads=[wupB[kc], u2TB[s]] + u2TQ[s], writes=[psB[bv_]], signal=(kc == KC - 1),
                                     out=ps[bv_][:, 0:TT], lhsT=wup[:, kc, DFF + fc * P:DFF + (fc + 1) * P],
                                     rhs=u2T[s][:, kc, 2:TT + 2], start=(kc == 0), stop=(kc == KC - 1))
                            cw = lambda jj: pv[:, C_CW + fc * 3 + jj:C_CW + fc * 3 + jj + 1]
                            k.op("dve", "tensor_scalar", reads=[psB[bg], pvB], writes=[cbufB[pr]], out=cbuf[pr][:],
                                 in0=ps[bg][:, 2:TT + 2], scalar1=cw(2), scalar2=pv[:, C_CB + fc:C_CB + fc + 1], op0=ALU.mult,
                                 op1=ALU.add)
                            k.op("dve", "scalar_tensor_tensor", reads=[psB[bg], pvB], writes=[cbufB[pr]], out=cbuf[pr][:],
                                 in0=ps[bg][:, 1:TT + 1], scalar=cw(1), in1=cbuf[pr][:], op0=ALU.mult, op1=ALU.add)
                            k.op("dve", "scalar_tensor_tensor", reads=[psB[bg], pvB], writes=[cbufB[pr]], out=cbuf[pr][:],
                                 in0=ps[bg][:, 0:TT], scalar=cw(0), in1=cbuf[pr][:], op0=ALU.mult, op1=ALU.add)
                            k.op("act", "activation", reads=[psB[bv_]], writes=[vbB[pr]], out=vb[pr][:], in_=ps[bv_][:, 0:TT], func=AF.Copy)
                            k.op("act", "activation", reads=[cbufB[pr]], writes=[geB[pr]], out=ge[pr][:], in_=cbuf[pr][:], func=AF.Gelu)
                            k.op("pool", "tensor_tensor", reads=[geB[pr], vbB[pr]], writes=[hidB[fc]], out=hid[:, fc, :], in0=vb[pr][:],
                                 in1=ge[pr][:], op=ALU.mult)
                        if j + 1 < T // TT:
                            pro_chain(j + 1)
                        epi = []
                        for sub in range(2):
                            ti = 2 * j + sub
                            o = octr % 2
                            octr += 1
                            for half in range(2):
                                bk = 4 + half
                                for fc in range(NFC):
                                    k.op("pe", "matmul", reads=[hidB[fc], wdnB[fc]], writes=[psB[bk]], signal=(fc == NFC - 1),
                                         out=ps[bk][:], lhsT=hid[:, fc, sub * P:(sub + 1) * P], rhs=wdn[:, fc, half * 512:(half + 1) * 512],
                                         start=(fc == 0), stop=(fc == NFC - 1))
                                k.op("dve", "tensor_tensor", reads=[psB[bk], hsB[j % 3]], writes=[osbB[o]],
                                     out=osb[o][:, half * 512:(half + 1) * 512], in0=ps[bk][:],
                                     in1=hs[j % 3][:, sub, half * 512:(half + 1) * 512], op=ALU.add)
                            k.op("act", "activation", reads=[osbB[o]], writes=[junkEB, stEB[NT + ti]], out=junkE[:], in_=osb[o][:],
                                 func=AF.Square, accum_out=stE[:, 3 * NT + ti:3 * NT + ti + 1])
                            rstd_chain(stE[:, 3 * NT + ti:3 * NT + ti + 1], stE[:, 4 * NT + ti:4 * NT + ti + 1],
                                       stE[:, 5 * NT + ti:5 * NT + ti + 1], D, [stEB[NT + ti]], stEB[NT + ti])
                            epi.append((ti, o))
                        if j + 1 < T // TT:
                            pro_pe(j + 1)
                        for ti, o in epi:
                            k.op("dve", "scalar_tensor_tensor", reads=[osbB[o], stEB[NT + ti], gFB], writes=[osbB[o]], out=osb[o][:],
                                 in0=osb[o][:], scalar=stE[:, 5 * NT + ti:5 * NT + ti + 1], in1=gF[:], op0=ALU.mult, op1=ALU.mult)
                            k.dma("pool", out_d[ti * P:(ti + 1) * P, :], osb[o][:], osbD[o], reads=[osbB[o]], is_out=True)
                    k.barrier()

        except Stop:
            pass
        k.mute = False
        k.barrier()

        with nc.Block() as block:
            @block.sync
            def _(h):
                k.replay("sp", h)

            @block.tensor
            def _(h):
                k.replay("pe", h)

            @block.scalar
            def _(h):
                k.replay("act", h)

            @block.vector
            def _(h):
                k.replay("dve", h)

            @block.gpsimd
            def _(h):
                k.replay("pool", h)
    build.marks = k.marks
    return nc


def _col(v):
    v = np.asarray(v, np.float32).reshape(-1, P)
    return np.ascontiguousarray(v.T)


def _prep(inputs):
    f = lambda a: np.ascontiguousarray(np.asarray(a, np.float32))
    pvec = np.zeros((P, 128), np.float32)
    pvec[:, 0:8] = _col(inputs["norm1_g"][0])
    pvec[:, 8:16] = _col(np.concatenate([np.asarray(inputs["attn_norm_g"][0]), np.asarray(inputs["hgrn_norm_g"][0])]))
    pvec[:, 16:24] = _col(inputs["norm2_g"][0])
    lbl = np.asarray(inputs["hgrn_lb_logits"], np.float32)
    pvec[:, 24:28] = _col(lbl[0])
    pvec[:, 28:32] = _col(lbl[1])
    cw = np.asarray(inputs["conv_w"][0], np.float32)
    for j in range(3):
        pvec[:, 32 + j:32 + 66:3] = _col(cw[j])
    pvec[:, 98:120] = _col(inputs["conv_b"][0])
    shared = {"w_in": f(inputs["w_in"][0]), "w_out": f(inputs["w_out"][0]), "w_up": f(inputs["w_up"][0]),
              "w_down": f(inputs["w_down"][0]), "pvec": pvec, "gfin": f(inputs["final_norm_g"])}
    x = f(inputs["x"])
    return [dict(shared, x=x[i]) for i in range(NCORES)]


def kernel(**inputs):
    nc = build()
    in_maps = _prep(inputs)
    res = run_bass_kernel_spmd(nc, in_maps, core_ids=list(range(NCORES)))
    return np.stack([np.asarray(r["out"], np.float32) for r in res.results], axis=0)
```

```python
import numpy as np
from contextlib import ExitStack
import concourse.bass as bass
import concourse.mybir as mybir
from concourse.bass_utils import run_bass_kernel_spmd

F32 = mybir.dt.float32
BF16 = mybir.dt.bfloat16
AF = mybir.ActivationFunctionType
ALU = mybir.AluOpType

P = 128
T = 4096
D = 1024
KC = 8
NT = T // P
DFF = 2816
NFC = DFF // P
EPS = 1e-6
NEG = -30000.0
NCORES = 8


class Ev:
    __slots__ = ("sem", "val", "pe")

    def __init__(self, sem, val, pe=False):
        self.sem, self.val, self.pe = sem, val, pe


class Buf:
    __slots__ = ("name", "w", "r")

    def __init__(self, name=""):
        self.name, self.w, self.r = name, None, {}


class Eng:
    def __init__(self, name, sem, is_pe=False):
        self.name, self.sem, self.is_pe = name, sem, is_pe
        self.cnt = 0
        self.ops = []
        self.waited = {}
        self.pending = []


class DSem:
    def __init__(self, sem):
        self.sem, self.cnt = sem, 0
        self.nobarrier = False


class K:
    def __init__(self, nc, es):
        self.nc, self.es = nc, es
        self.eng = {}
        for n, pe in (("pe", True), ("act", False), ("dve", False), ("pool", False), ("sp", False)):
            self.eng[n] = Eng(n, es.enter_context(nc.semaphore("sem_" + n)), pe)
        self.nds = 0
        self.out_evs = []
        self.dsems = []
        self.mute = False
        self.marks = []

    def dsem(self):
        self.nds += 1
        d = DSem(self.es.enter_context(self.nc.semaphore("dsem%d" % self.nds)))
        self.dsems.append(d)
        return d

    def _wait(self, e, ev):
        if ev is None:
            return
        if e.is_pe and ev.pe:
            return
        assert ev.val is not None, "unresolved event"
        k = id(ev.sem)
        if e.waited.get(k, 0) >= ev.val:
            return
        e.waited[k] = ev.val
        e.ops.append(("w", ev.sem, ev.val))

    def _deps(self, e, reads, writes):
        for b in reads:
            self._wait(e, b.w)
        for b in writes:
            self._wait(e, b.w)
            for ev in b.r.values():
                self._wait(e, ev)

    def _mark(self, ev, reads, writes):
        for b in reads:
            b.r[id(ev.sem)] = ev
        for b in writes:
            b.w = ev
            b.r = {}

    def op(self, en, method, reads=(), writes=(), signal=True, **kw):
        if self.mute:
            return None
        e = self.eng[en]
        self._deps(e, reads, writes)
        ev = Ev(e.sem, None, e.is_pe)
        if signal:
            e.cnt += 1
            ev.val = e.cnt
            for p in e.pending:
                p.val = e.cnt
            e.pending = []
        else:
            assert e.is_pe
            e.pending.append(ev)
        e.ops.append(("op", method, kw, signal))
        self._mark(ev, reads, writes)
        return ev

    def dma(self, en, out, in_, ds, reads=(), writes=(), is_out=False):
        if self.mute:
            return None
        e = self.eng[en]
        self._deps(e, reads, writes)
        ds.cnt += 16
        ev = Ev(ds.sem, ds.cnt)
        e.ops.append(("dma", out, in_, ds.sem))
        self._mark(ev, reads, writes)
        if is_out:
            self.out_evs.append(ev)
        return ev

    def mark(self, name):
        self.marks.append((name, {n: sum(1 for o in e.ops if o[0] == 'op') for n, e in self.eng.items()}))

    def barrier(self):
        if self.mute:
            return
        names = ("pe", "act", "dve", "pool")
        pe = self.eng["pe"]
        if pe.pending:
            pe.cnt += 1
            for p in pe.pending:
                p.val = pe.cnt
            pe.pending = []
            pe.ops.append(("sig",))
        for en in names + ("sp",):
            e = self.eng[en]
            for on in names:
                o = self.eng[on]
                if o is e or o.cnt == 0:
                    continue
                if e.waited.get(id(o.sem), 0) < o.cnt:
                    e.waited[id(o.sem)] = o.cnt
                    e.ops.append(("w", o.sem, o.cnt))
            for ds in self.dsems:
                if ds.nobarrier:
                    continue
                if ds.cnt and e.waited.get(id(ds.sem), 0) < ds.cnt:
                    e.waited[id(ds.sem)] = ds.cnt
                    e.ops.append(("w", ds.sem, ds.cnt))

    def replay(self, en, h):
        e = self.eng[en]
        for o in e.ops:
            if o[0] == "w":
                h.wait_ge(o[1], o[2])
            elif o[0] == "sig":
                h.drain().then_inc(e.sem, 1)
            elif o[0] == "op":
                ins = getattr(h, o[1])(**o[2])
                if o[3]:
                    ins.then_inc(e.sem, 1)
            else:
                h.dma_start(out=o[1], in_=o[2]).then_inc(o[3], 16)


class Stop(Exception):
    pass


def build(debug=False, stop=None):
    nc = bass.Bass("TRN2", target_bir_lowering=False)
    x_d = nc.dram_tensor("x", [T, D], F32, kind="ExternalInput").ap()
    win_d = nc.dram_tensor("w_in", [D, 3584], F32, kind="ExternalInput").ap()
    wout_d = nc.dram_tensor("w_out", [D, D], F32, kind="ExternalInput").ap()
    wup_d = nc.dram_tensor("w_up", [D, 2 * DFF], F32, kind="ExternalInput").ap()
    wdn_d = nc.dram_tensor("w_down", [DFF, D], F32, kind="ExternalInput").ap()
    pv_d = nc.dram_tensor("pvec", [P, 128], F32, kind="ExternalInput").ap()
    gf_d = nc.dram_tensor("gfin", [D], F32, kind="ExternalInput").ap()
    out_d = nc.dram_tensor("out", [T, D], F32, kind="ExternalOutput").ap()
    mixed_d = nc.dram_tensor("mixed_scr", [D, T], BF16, kind="ExternalOutput" if debug else "Internal").ap()
    h_d = nc.dram_tensor("h_scr", [T, D], F32, kind="ExternalOutput" if debug else "Internal").ap()
    C_G1, C_GMIX, C_G2, C_LBL, C_CW, C_CB = 0, 8, 16, 24, 32, 98

    with ExitStack() as es:
        k = K(nc, es)
        sb = lambda n, s, d, st=es: st.enter_context(nc.sbuf_tensor(n, s, d))
        ps = [es.enter_context(nc.psum_tensor("ps%d" % i, [P, 512], F32)) for i in range(8)]
        psB = [Buf("ps%d" % i) for i in range(8)]
        ident = sb("ident", [P, P], BF16)
        identf = sb("identf", [P, P], F32)
        zf = sb("zf", [P, 256], F32)
        hmaskf = sb("hmaskf", [P, P], F32)
        hmask = sb("hmask", [P, P], BF16)
        ones = sb("ones", [P, P], BF16)
        maskp = sb("maskp", [P, 2, P], BF16)
        resetm = sb("resetm", [P, 1024], F32)
        pv = sb("pv", [P, 128], F32)
        lbt = sb("lbt", [P, 16], F32)
        nhalf = sb("nhalf", [P, 1], F32)
        gF = sb("gF", [P, D], F32)
        cB = Buf("consts")
        mixB = [Buf("mix%d" % i) for i in range(8)]
        pvB = Buf("pv")
        gFB = Buf("gF")

        k.dma("sp", pv[:], pv_d[:, :], k.dsem(), writes=[pvB])
        k.dma("sp", gF[:], gf_d.partition_broadcast(P), k.dsem(), writes=[gFB])
        k.op("pool", "memset", writes=[cB], ap=identf[:], constant=1.0)
        k.op("pool", "affine_select", reads=[cB], writes=[cB], out=identf[:], in_=identf[:], pattern=[[-1, P]],
             compare_op=ALU.is_equal, fill=0.0, base=0, channel_multiplier=1)
        k.op("pool", "tensor_copy", reads=[cB], writes=[cB], out=ident[:], in_=identf[:])
        k.op("pool", "memset", reads=[cB], writes=[cB], ap=zf[:], constant=1.0)
        k.op("pool", "memset", reads=[cB], writes=[cB], ap=hmaskf[:], constant=1.0)
        k.op("pool", "affine_select", reads=[cB], writes=[cB], out=hmaskf[:], in_=hmaskf[:], pattern=[[1, P]],
             compare_op=ALU.is_ge, fill=0.0, base=0, channel_multiplier=-1)
        k.op("pool", "memset", reads=[cB], writes=[cB], ap=hmaskf[0:64, 64:128], constant=0.0)
        k.op("pool", "tensor_copy", reads=[cB], writes=[cB], out=hmask[:], in_=hmaskf[:])
        k.op("pool", "memset", reads=[cB], writes=[cB], ap=ones[:], constant=1.0)
        k.op("pool", "affine_select", reads=[cB], writes=[cB], out=maskp[:], in_=zf[:].rearrange("p (h q) -> p h q", h=2),
             pattern=[[0, 2], [-1, P]],
             compare_op=ALU.is_ge, fill=0.0, base=0, channel_multiplier=1)
        k.op("pool", "memset", reads=[cB], writes=[cB], ap=resetm[:], constant=1.0)
        k.op("pool", "memset", reads=[cB], writes=[cB],
             ap=resetm[:].rearrange("p (c k) -> p c k", k=64)[:, :, 0:1], constant=0.0)
        k.op("pool", "memset", reads=[cB], writes=[cB], ap=nhalf[:], constant=-0.5)
        k.op("dve", "tensor_tensor", reads=[pvB], writes=[cB], out=lbt[:, 8:12], in0=pv[:, C_LBL:C_LBL + 4],
             in1=pv[:, C_LBL + 4:C_LBL + 8], op=ALU.subtract)
        k.op("act", "activation", reads=[cB], writes=[cB], out=lbt[:, 0:4], in_=lbt[:, 8:12], func=AF.Sigmoid)
        k.op("dve", "tensor_scalar", reads=[cB], writes=[cB], out=lbt[:, 4:8], in0=lbt[:, 0:4], scalar1=-1.0,
             scalar2=1.0, op0=ALU.mult, op1=ALU.add)

        def rstd_chain(ss_ap, ssn_ap, rstd_ap, n, bufs_r, buf_w, src_psum=False):
            k.op("dve", "tensor_scalar", reads=bufs_r, writes=[buf_w], out=ssn_ap, in0=ss_ap, scalar1=1.0 / n,
                 scalar2=EPS, op0=ALU.mult, op1=ALU.add)
            k.op("pool", "tensor_tensor", reads=[buf_w, cB], writes=[buf_w], out=rstd_ap, in0=ssn_ap,
                 in1=nhalf[:, 0:1], op=ALU.pow)

        try:
            with ExitStack() as ms:
                msb = lambda n, s, d: sb(n, s, d, ms)
                xT = msb("xT", [P, KC, T], BF16)
                xTB = [Buf("xT%d" % i) for i in range(NT)]
                wst = [msb("wst%d" % i, [P, KC, P], F32) for i in range(2)]
                wstB = [Buf() for _ in range(2)]
                wstD = [k.dsem() for _ in range(2)]
                wbf = [[msb("wbf%d_%d" % (g, j), [P, KC, P], BF16) for j in range(4)] for g in range(2)]
                wbfB = [[Buf() for _ in range(4)] for _ in range(2)]
                win_v = win_d.rearrange("(kc p) n -> p kc n", p=P)
                wctr = [0]

                def load_wblock(gslot, j, col0):
                    s = wctr[0] % 2
                    wctr[0] += 1
                    k.dma("sp", wst[s][:], win_v[:, :, col0:col0 + P], wstD[s], writes=[wstB[s]])
                    for kc in range(KC):
                        k.op("pool", "tensor_scalar", reads=[wstB[s], pvB], writes=[wbfB[gslot][j]],
                             out=wbf[gslot][j][:, kc, :], in0=wst[s][:, kc, :], scalar1=pv[:, C_G1 + kc:C_G1 + kc + 1],
                             scalar2=1.0, op0=ALU.mult, op1=ALU.mult)

                groups = []
                for hp in range(4):
                    groups.append(("attn", hp, [128 * hp, 512 + 128 * hp, 1024 + 128 * hp]))
                for hh in range(4):
                    groups.append(("hgrn", hh, [1536 + 128 * hh, 2048 + 128 * hh, 2560 + 128 * hh, 3072 + 128 * hh]))

                def load_group(gi):
                    if gi >= len(groups):
                        return
                    for j, c0 in enumerate(groups[gi][2]):
                        load_wblock(gi % 2, j, c0)

                load_group(0)

                with ExitStack() as pa:
                    NXS = 8
                    xin = [sb("xin%d" % i, [P, D], F32, pa) for i in range(NXS)]
                    xinB = [Buf() for _ in range(NXS)]
                    xinD = [k.dsem() for _ in range(NXS)]
                    junk = [sb("junk%d" % i, [P, D], BF16, pa) for i in range(2)]
                    junkB = [Buf() for _ in range(2)]
                    xb = [sb("xb%d" % i, [P, D], BF16, pa) for i in range(2)]
                    xbB = [Buf() for _ in range(2)]
                    st = sb("stA", [P, 3 * NT], F32, pa)
                    stB = [Buf() for _ in range(NT)]
                    for i in range(NT):
                        s = i % NXS
                        k.dma("sp", xin[s][:], x_d[i * P:(i + 1) * P, :], xinD[s], writes=[xinB[s]])
                        k.op("act", "activation", reads=[xinB[s]], writes=[junkB[i % 2], stB[i]], out=junk[i % 2][:],
                             in_=xin[s][:], func=AF.Square, accum_out=st[:, i:i + 1])
                        rstd_chain(st[:, i:i + 1], st[:, NT + i:NT + i + 1], st[:, 2 * NT + i:2 * NT + i + 1], D,
                                   [stB[i]], stB[i])
                        k.op("dve", "tensor_scalar", reads=[xinB[s], stB[i]], writes=[xbB[i % 2]], out=xb[i % 2][:],
                             in0=xin[s][:], scalar1=st[:, 2 * NT + i:2 * NT + i + 1], scalar2=None, op0=ALU.mult)
                        bp = (i % 4) * 2
                        for kc in range(KC):
                            bkk = bp + kc // 4
                            k.op("pe", "matmul", reads=[xbB[i % 2], cB], writes=[psB[bkk]], signal=(kc % 4 == 3),
                                 out=ps[bkk][:, (kc % 4) * P:(kc % 4 + 1) * P], lhsT=xb[i % 2][:, kc * P:(kc + 1) * P],
                                 rhs=ident[:], start=True, stop=True, skip_group_check=True)
                        k.op("act", "activation", reads=[psB[bp]], writes=[xTB[i]], out=xT[:, 0:4, i * P:(i + 1) * P],
                             in_=ps[bp][:].rearrange("p (c t) -> p c t", t=P), func=AF.Copy)
                        k.op("dve", "tensor_copy", reads=[psB[bp + 1]], writes=[xTB[i]], out=xT[:, 4:8, i * P:(i + 1) * P],
                             in_=ps[bp + 1][:].rearrange("p (c t) -> p c t", t=P))
                k.barrier()
                k.mark('A')
                if stop == 'A':
                    k.mute = True
                def proj_fm(wt, wB, c, bank, tok0=None, ntok=512):
                    t0 = c * 512 if tok0 is None else tok0
                    tiles = sorted(set(range(t0 // P, (t0 + ntok - 1) // P + 1)))
                    for kc in range(KC):
                        k.op("pe", "matmul", reads=[wB] + [xTB[t] for t in tiles], writes=[psB[bank]],
                             signal=(kc == KC - 1), out=ps[bank][:, 0:ntok], lhsT=wt[:, kc, :],
                             rhs=xT[:, kc, t0:t0 + ntok], start=(kc == 0), stop=(kc == KC - 1))

                with ExitStack() as pb:
                    bsb = lambda n, s, d: sb(n, s, d, pb)
                    qT2 = bsb("qT2", [P, 2, T], BF16)
                    qzB = Buf("qzero")
                    k.op("pool", "memset", writes=[qzB], ap=qT2[64:128, 0, :], constant=0.0)
                    k.op("pool", "memset", writes=[qzB], ap=qT2[0:64, 1, :], constant=0.0)
                    kT = bsb("kT", [P, T], BF16)
                    VT = bsb("VT", [P, T], BF16)
                    VTB = [Buf() for _ in range(8)]
                    qB = [Buf() for _ in range(8)]
                    kB = [Buf() for _ in range(8)]
                    DILS = (1, 4, 16)
                    V = [bsb("V%d" % di, [P, NT, P], BF16) for di in range(3)]
                    VB = [[Buf() for _ in range(NT)] for _ in range(3)]
                    pT = [bsb("pT%d" % i, [P, 2, 256], BF16) for i in range(5)]
                    pTB = [Buf() for _ in range(5)]
                    pTBp = [Buf() for _ in range(5)]
                    Oacc = bsb("Oacc", [P, T], F32)
                    Lacc = bsb("Lacc", [P, T], F32)
                    OB = [Buf() for _ in range(8)]
                    LB_ = [Buf() for _ in range(8)]
                    rl = [bsb("rl%d" % i, [P, 512], F32) for i in range(2)]
                    rlB = [Buf() for _ in range(2)]
                    at = [bsb("at%d" % i, [P, 512], BF16) for i in range(2)]
                    atB = [Buf() for _ in range(2)]
                    atD = [k.dsem() for _ in range(2)]
                    SB_BANKS = (0, 1, 6, 7)
                    O_BANKS = ((2, 3), (4, 5))
                    PJ_BANKS = (0, 1)
                    sctr = [0]
                    pjc = [0]
                    def hp_body(gi):
                        hp = groups[gi][1]
                        gs = gi % 2
                        load_group(gi + 1)
                        wq, wk, wv = wbf[gs][0], wbf[gs][1], wbf[gs][2]
                        for c in range(8):
                            for which in ("q", "k"):
                                bank = PJ_BANKS[pjc[0] % 2]
                                pjc[0] += 1
                                if which == "q":
                                    proj_fm(wq, wbfB[gs][0], c, bank)
                                    k.op("act", "activation", reads=[psB[bank]], writes=[qB[c]],
                                         out=qT2[0:64, 0, c * 512:(c + 1) * 512], in_=ps[bank][0:64, :], func=AF.Copy, scale=0.125)
                                    k.op("dve", "tensor_scalar", reads=[psB[bank]], writes=[qB[c]],
                                         out=qT2[64:128, 1, c * 512:(c + 1) * 512], in0=ps[bank][64:128, :], scalar1=0.125,
                                         scalar2=None, op0=ALU.mult)
                                else:
                                    proj_fm(wk, wbfB[gs][1], c, bank)
                                    if c % 2:
                                        k.op("act", "activation", reads=[psB[bank]], writes=[kB[c]],
                                             out=kT[:, c * 512:(c + 1) * 512], in_=ps[bank][:], func=AF.Copy)
                                    else:
                                        k.op("dve", "tensor_copy", reads=[psB[bank]], writes=[kB[c]],
                                             out=kT[:, c * 512:(c + 1) * 512], in_=ps[bank][:])
                        for c in range(8):
                            bank = PJ_BANKS[pjc[0] % 2]
                            pjc[0] += 1
                            proj_fm(wv, wbfB[gs][2], c, bank)
                            if c % 2:
                                k.op("act", "activation", reads=[psB[bank]], writes=[VTB[c]],
                                     out=VT[:, c * 512:(c + 1) * 512], in_=ps[bank][:], func=AF.Copy)
                            else:
                                k.op("dve", "tensor_copy", reads=[psB[bank]], writes=[VTB[c]],
                                     out=VT[:, c * 512:(c + 1) * 512], in_=ps[bank][:])
                        yield
                        for di, d in enumerate(DILS):
                            nb = NT // d
                            for b0 in range(0, NT, 4):
                                bank = PJ_BANKS[pjc[0] % 2]
                                pjc[0] += 1
                                for j in range(4):
                                    b = b0 + j
                                    r, n = b // nb, b % nb
                                    t0 = n * P * d
                                    chs = sorted(set(t // 4 for t in range(t0 // P, t0 // P + d)))
                                    lt = VT[:, t0:t0 + P * d].rearrange("p (i s) -> p i s", s=d)[:, :, r]
                                    k.op("pe", "matmul", reads=[cB] + [VTB[c] for c in chs], writes=[psB[bank]], signal=(j == 3),
                                         out=ps[bank][:, j * P:(j + 1) * P], lhsT=lt, rhs=ident[:], start=True, stop=True,
                                         skip_group_check=True)
                                eng = "act" if pjc[0] % 2 else "dve"
                                kw = {"func": AF.Copy} if eng == "act" else {}
                                k.op(eng, "activation" if eng == "act" else "tensor_copy", reads=[psB[bank]],
                                     writes=[VB[di][b0 + j] for j in range(4)], out=V[di][:, b0:b0 + 4, :],
                                     in_=ps[bank][:].rearrange("p (j f) -> p j f", f=P), **kw)
                        yield
                        fresh = [True, True]

                        def stage1(di, d, b):
                            nb = NT // d
                            r, n = b // nb, b % nb
                            has_next = (n + 1 < nb)
                            nq = 256 if has_next else 128
                            t0 = n * P * d
                            sbk = SB_BANKS[sctr[0] % 4]
                            pslot = sctr[0] % 5
                            sctr[0] += 1
                            sview = ps[sbk][:, 0:2 * nq].rearrange("p (h q) -> p h q", h=2)
                            ktiles = list(range(t0 // P, t0 // P + d))
                            qtiles = list(range(t0 // P, min(NT, t0 // P + (2 if has_next else 1) * d)))
                            kchunks = sorted(set(t // 4 for t in ktiles))
                            qchunks = sorted(set(t // 4 for t in qtiles))
                            lt = kT[:, t0:t0 + P * d].rearrange("p (i s) -> p i s", s=d)[:, :, r]
                            rt = qT2[:, :, t0:t0 + nq * d].rearrange("p h (i s) -> p h i s", s=d)[:, :, :, r]
                            k.op("pe", "matmul", reads=[kB[c] for c in kchunks] + [qB[c] for c in qchunks] + [qzB],
                                 writes=[psB[sbk]], out=sview, lhsT=lt, rhs=rt, start=True, stop=True)
                            k.op("act", "activation", reads=[psB[sbk]], writes=[pTB[pslot]] + ([pTBp[pslot]] if has_next else []),
                                 out=pT[pslot][:, :, 0:nq], in_=sview, func=AF.Exp)
                            k.op("pool", "affine_select", reads=[pTB[pslot]], writes=[pTB[pslot]], out=pT[pslot][:, :, 0:P],
                                 in_=pT[pslot][:, :, 0:P], pattern=[[0, 2], [1, P]], compare_op=ALU.is_ge, fill=0.0, base=0,
                                 channel_multiplier=-1)
                            if has_next:
                                k.op("dve", "tensor_tensor", reads=[pTBp[pslot], cB], writes=[pTBp[pslot]],
                                     out=pT[pslot][:, :, P:2 * P], in0=pT[pslot][:, :, P:2 * P], in1=maskp[:], op=ALU.mult)
                            return (di, d, b, has_next, pslot)

                        def stage2(item):
                            di, d, b, has_next, pslot = item
                            cch = b // 4
                            par = cch % 2
                            bn, bl = O_BANKS[par]
                            col = (b % 4) * P
                            parts = [(0, P, par, col)]
                            if has_next:
                                if b % 4 == 3:
                                    parts.append((P, P, 1 - par, 0))
                                else:
                                    parts[0] = (0, 256, par, col)
                            for pi, (pc0, wd, pr, oc) in enumerate(parts):
                                bn_, bl_ = O_BANKS[pr]
                                for h in range(2):
                                    hs_ = slice(h * 64, (h + 1) * 64)
                                    first = fresh[pr]
                                    prd = ([pTB[pslot]] if pc0 == 0 else []) + ([pTBp[pslot]] if pc0 + wd > P else [])
                                    k.op("pe", "matmul", reads=[VB[di][b]] + prd, writes=[psB[bn_]], signal=False,
                                         out=ps[bn_][hs_, oc:oc + wd], lhsT=V[di][:, b, hs_],
                                         rhs=pT[pslot][:, h, pc0:pc0 + wd], start=first, stop=True,
                                         skip_group_check=True)
                                    k.op("pe", "matmul", reads=[cB] + prd, writes=[psB[bl_]],
                                         signal=(h == 1), out=ps[bl_][hs_, oc:oc + wd], lhsT=ones[:, 0:64],
                                         rhs=pT[pslot][:, h, pc0:pc0 + wd], start=first, stop=True,
                                         skip_group_check=True)
                                fresh[pr] = False
                            if b % 4 == 3:
                                if d == 1:
                                    do, dl = Oacc[:, cch * 512:(cch + 1) * 512], Lacc[:, cch * 512:(cch + 1) * 512]
                                    so, sl = ps[bn][:], ps[bl][:]
                                    chs = [cch]
                                elif d == 4:
                                    rr, half = cch // 2, cch % 2
                                    v = lambda A: A[:, half * 2048:(half + 1) * 2048].rearrange("p (m s) -> p m s", s=4)[:, :, rr]
                                    do, dl = v(Oacc), v(Lacc)
                                    so, sl = ps[bn][:], ps[bl][:]
                                    chs = list(range(half * 4, half * 4 + 4))
                                else:
                                    v = lambda A: A[:, :].rearrange("p (m s) -> p m s", s=16)[:, :, 2 * cch:2 * cch + 2]
                                    do, dl = v(Oacc), v(Lacc)
                                    so = ps[bn][:].rearrange("p (r m) -> p m r", r=2)
                                    sl = ps[bl][:].rearrange("p (r m) -> p m r", r=2)
                                    chs = list(range(8))
                                if d == 1:
                                    k.op("dve", "tensor_copy", reads=[psB[bn]], writes=[OB[c] for c in chs], out=do, in_=so)
                                    k.op("act", "activation", reads=[psB[bl]], writes=[LB_[c] for c in chs], out=dl, in_=sl,
                                         func=AF.Copy)
                                else:
                                    k.op("dve", "tensor_tensor", reads=[psB[bn]], writes=[OB[c] for c in chs], out=do,
                                         in0=so, in1=do, op=ALU.add)
                                    k.op("dve", "tensor_tensor", reads=[psB[bl]], writes=[LB_[c] for c in chs], out=dl,
                                         in0=sl, in1=dl, op=ALU.add)
                                fresh[par] = True

                        work = [(di, d, b) for di, d in enumerate(DILS) for b in range(NT)]
                        q_items = [stage1(*work[0]), stage1(*work[1]), stage1(*work[2])]
                        for wi in range(len(work)):
                            if wi + 3 < len(work):
                                q_items.append(stage1(*work[wi + 3]))
                            stage2(q_items.pop(0))
                        yield
                        for c in range(8):
                            k.op("act", "activation", reads=[LB_[c]], writes=[LB_[c]], out=Lacc[:, c * 512:(c + 1) * 512],
                                 in_=Lacc[:, c * 512:(c + 1) * 512], func=AF.Ln)
                        for c in range(8):
                            s = c % 2
                            k.op("act", "activation", reads=[LB_[c]], writes=[rlB[s]], out=rl[s][:],
                                 in_=Lacc[:, c * 512:(c + 1) * 512], func=AF.Exp, scale=-1.0)
                            k.op("dve", "tensor_tensor", reads=[OB[c], rlB[s]], writes=[atB[s]], out=at[s][:],
                                 in0=Oacc[:, c * 512:(c + 1) * 512], in1=rl[s][:], op=ALU.mult)
                            k.dma("pool", mixed_d[hp * P:(hp + 1) * P, c * 512:(c + 1) * 512], at[s][:], atD[s],
                                  reads=[atB[s]], writes=[mixB[hp]])
                    gens = [hp_body(gi) for gi in range(4)]
                    next(gens[0])
                    next(gens[0])
                    for gi in range(4):
                        next(gens[gi])
                        if gi + 1 < 4:
                            next(gens[gi + 1])
                        next(gens[gi], None)
                        if gi + 1 < 4:
                            next(gens[gi + 1])
                k.barrier()
                k.mark('B')
                if stop == 'B':
                    k.mute = True
                with ExitStack() as pc:
                    csb = lambda n, s, d: sb(n, s, d, pc)
                    SEG = 1024
                    f32b = {}
                    for nm in ("SG", "SQ", "QF", "S3", "FG", "KY", "LF", "Bc", "ENB", "BD", "EKD", "REC", "LN1", "RS", "R1"):
                        f32b[nm] = (csb("h_" + nm, [P, SEG], F32), Buf(nm))
                    A = lambda nm: f32b[nm][0]
                    Bf = lambda nm: f32b[nm][1]
                    dbl = {}
                    for nm, dt_ in (("QD", BF16), ("KI", BF16), ("KE", BF16), ("EB", F32), ("GT", F32)):
                        dbl[nm] = ([csb("h_%s%d" % (nm, i), [P, SEG], dt_) for i in range(2)], [Buf(nm) for _ in range(2)])
                    RSQ, RSQB = csb("h_RSQ", [P, SEG], BF16), Buf("RSQ")
                    MX = [csb("h_MX%d" % i, [P, SEG], BF16) for i in range(2)]
                    MXB = [Buf() for _ in range(2)]
                    MXD = [k.dsem() for _ in range(2)]
                    IVs = [csb("h_IV%d" % i, [P, 8, P], BF16) for i in range(2)]
                    IVBs = [[Buf() for _ in range(2)] for _ in range(2)]
                    Ke8 = csb("h_Ke8", [P, 8, P], BF16)
                    Ke8B = [Buf() for _ in range(8)]
                    As8 = csb("h_As8", [P, 8, P], BF16)
                    As8B = [Buf() for _ in range(8)]
                    Sf = [csb("h_Sf%d" % i, [P, P], F32) for i in range(2)]
                    SfB = [Buf() for _ in range(2)]
                    S16 = csb("h_S16", [P, 17, P], BF16)
                    S16B = [Buf() for _ in range(17)]
                    mxc = [0]
                    sidx = [0]
                    pjb = [0]

                    def stageA(gi, hh, sg, sl):
                        gs = gi % 2
                        if sg == 0:
                            load_group(gi + 1)
                        wq, wf, wi, wg = wbf[gs]
                        wqB, wfB, wiB, wgB = wbfB[gs]
                        lbc, omlc = lbt[:, hh:hh + 1], lbt[:, 4 + hh:5 + hh]
                        QD, KI, KE, EB, GT = (dbl[n][0][sl] for n in ("QD", "KI", "KE", "EB", "GT"))
                        QDB, KIB, KEB, EBB, GTB = (dbl[n][1][sl] for n in ("QD", "KI", "KE", "EB", "GT"))
                        IV, IVB = IVs[sl], IVBs[sl]

                        def nbank():
                            pjb[0] += 1
                            return (0, 1, 6, 7)[pjb[0] % 4]
                        for cc in range(2):
                            cs = slice(cc * 512, (cc + 1) * 512)
                            bk = nbank()
                            proj_fm(wf, wfB, sg * 2 + cc, bk)
                            k.op("act", "activation", reads=[psB[bk]], writes=[Bf("SG")], out=A("SG")[:, cs], in_=ps[bk][:],
                                 func=AF.Sigmoid)
                            yield
                        for cc in range(2):
                            cs = slice(cc * 512, (cc + 1) * 512)
                            bk = nbank()
                            proj_fm(wq, wqB, sg * 2 + cc, bk)
                            k.op("act", "activation", reads=[psB[bk]], writes=[Bf("QF")], out=A("QF")[:, cs], in_=ps[bk][:],
                                 func=AF.Silu)
                            yield
                        for cc in range(2):
                            cs = slice(cc * 512, (cc + 1) * 512)
                            bk = nbank()
                            proj_fm(wg, wgB, sg * 2 + cc, bk)
                            k.op("act", "activation", reads=[psB[bk]], writes=[GTB], out=GT[:, cs], in_=ps[bk][:],
                                 func=AF.Silu)
                            yield
                        for half in range(2):
                            bank = nbank()
                            for j in range(4):
                                ti = sg * 8 + half * 4 + j
                                for kc in range(KC):
                                    k.op("pe", "matmul", reads=[wiB, xTB[ti]], writes=[psB[bank]],
                                         signal=(kc == KC - 1 and j == 3), out=ps[bank][:, j * P:(j + 1) * P],
                                         lhsT=xT[:, kc, ti * P:(ti + 1) * P], rhs=wi[:, kc, :], start=(kc == 0),
                                         stop=(kc == KC - 1))
                            k.op("act", "activation", reads=[psB[bank]], writes=[IVB[half]],
                                 out=IV[:, half * 4:(half + 1) * 4, :], in_=ps[bank][:].rearrange("p (j f) -> p j f", f=P),
                                 func=AF.Copy)
                            yield
                        k.op("dve", "tensor_scalar", reads=[Bf("SG"), cB], writes=[Bf("FG")], out=A("FG")[:], in0=A("SG")[:],
                             scalar1=omlc, scalar2=lbc, op0=ALU.mult, op1=ALU.add)
                        yield
                        k.op("dve", "tensor_scalar", reads=[Bf("FG")], writes=[Bf("KY")], out=A("KY")[:], in0=A("FG")[:],
                             scalar1=-1.0, scalar2=1.0, op0=ALU.mult, op1=ALU.add)
                        yield
                        k.op("act", "activation", reads=[Bf("FG")], writes=[Bf("LF")], out=A("LF")[:], in_=A("FG")[:], func=AF.Ln)
                        yield
                        k.op("dve", "tensor_tensor_scan", reads=[Bf("LF"), cB], writes=[Bf("Bc")], out=A("Bc")[:],
                             data0=resetm[:], data1=A("LF")[:], initial=0.0, op0=ALU.mult, op1=ALU.add)
                        yield
                        k.op("act", "activation", reads=[Bf("Bc")], writes=[EBB], out=EB[:], in_=A("Bc")[:], func=AF.Exp)
                        yield
                        k.op("act", "activation", reads=[Bf("Bc")], writes=[Bf("ENB")], out=A("ENB")[:], in_=A("Bc")[:],
                             func=AF.Exp, scale=-1.0)
                        yield
                        bv = A("Bc")[:].rearrange("p (c k) -> p c k", k=64)
                        k.op("dve", "tensor_tensor", reads=[Bf("Bc")], writes=[Bf("BD")],
                             out=A("BD")[:].rearrange("p (c k) -> p c k", k=64),
                             in0=bv[:, :, 63:64].to_broadcast([P, SEG // 64, 64]), in1=bv, op=ALU.subtract)
                        yield
                        k.op("act", "activation", reads=[Bf("BD")], writes=[Bf("EKD")], out=A("EKD")[:], in_=A("BD")[:], func=AF.Exp)
                        yield
                        k.op("dve", "tensor_tensor", reads=[Bf("QF"), EBB], writes=[QDB], out=QD[:], in0=A("QF")[:],
                             in1=EB[:], op=ALU.mult)
                        yield
                        k.op("pool", "tensor_tensor", reads=[Bf("KY"), Bf("ENB")], writes=[KIB], out=KI[:], in0=A("KY")[:],
                             in1=A("ENB")[:], op=ALU.mult)
                        yield
                        k.op("pool", "tensor_tensor", reads=[Bf("KY"), Bf("EKD")], writes=[KEB], out=KE[:], in0=A("KY")[:],
                             in1=A("EKD")[:], op=ALU.mult)
                        yield

                    def stageB(gi, hh, sg, sl):
                        tk0 = sg * SEG
                        QD, KI, KE, EB, GT = (dbl[n][0][sl] for n in ("QD", "KI", "KE", "EB", "GT"))
                        QDB, KIB, KEB, EBB, GTB = (dbl[n][1][sl] for n in ("QD", "KI", "KE", "EB", "GT"))
                        IV, IVB = IVs[sl], IVBs[sl]
                        if sg == 0:
                            k.op("pool", "memset", writes=[SfB[0]], ap=Sf[0][:], constant=0.0)
                            k.op("pool", "memset", writes=[S16B[0]], ap=S16[:, 0, :], constant=0.0)
                            first_state = 0
                        else:
                            first_state = None
                        for tl in range(8):
                            ts_ = slice(tl * P, (tl + 1) * P)
                            a = tl % 2
                            k.op("pe", "matmul", reads=[KIB, QDB], writes=[psB[a]], out=ps[a][:, 0:P], lhsT=KI[:, ts_],
                                 rhs=QD[:, ts_], start=True, stop=True, skip_group_check=True)
                            k.op("pe", "matmul", reads=[KEB, cB], writes=[psB[6 + a]], out=ps[6 + a][:, 0:P], lhsT=KE[:, ts_],
                                 rhs=ident[:], start=True, stop=True, skip_group_check=True)
                            k.op("dve", "tensor_tensor", reads=[psB[a], cB], writes=[As8B[tl]], out=As8[:, tl, :], in0=ps[a][:, 0:P],
                                 in1=hmask[:], op=ALU.mult)
                            k.op("act", "activation", reads=[psB[6 + a]], writes=[Ke8B[tl]], out=Ke8[:, tl, :], in_=ps[6 + a][:, 0:P],
                                 func=AF.Copy)
                        for tl in range(8):
                            for c2 in range(2):
                                c = 2 * tl + c2
                                ub = 2 + c % 4
                                k.op("pe", "matmul", reads=[Ke8B[tl], IVB[tl // 4]], writes=[psB[ub]],
                                     out=ps[ub][:, (c // 4) * P:(c // 4 + 1) * P], lhsT=Ke8[c2 * 64:(c2 + 1) * 64, tl, :],
                                     rhs=IV[c2 * 64:(c2 + 1) * 64, tl, :], start=True, stop=True, skip_group_check=True)
                        if sg != 0:
                            k.op("dve", "tensor_copy", reads=[S16B[16]], writes=[S16B[0]], out=S16[:, 0, :], in_=S16[:, 16, :])
                        for c in range(16):
                            cur, nxt = sidx[0] % 2, (sidx[0] + 1) % 2
                            sidx[0] += 1
                            ub = 2 + c % 4
                            ecol = c * 64 + 63
                            k.op("dve", "scalar_tensor_tensor", reads=[SfB[cur], EBB, psB[ub]], writes=[SfB[nxt]],
                                 out=Sf[nxt][:], in0=Sf[cur][:], scalar=EB[:, ecol:ecol + 1],
                                 in1=ps[ub][:, (c // 4) * P:(c // 4 + 1) * P], op0=ALU.mult, op1=ALU.add)
                            k.op("dve", "tensor_copy", reads=[SfB[nxt]], writes=[S16B[c + 1]], out=S16[:, c + 1, :], in_=Sf[nxt][:])
                    def stageB2(gi, hh, sg, sl):
                        tk0 = sg * SEG
                        QD, KI, KE, EB, GT = (dbl[n][0][sl] for n in ("QD", "KI", "KE", "EB", "GT"))
                        QDB, KIB, KEB, EBB, GTB = (dbl[n][1][sl] for n in ("QD", "KI", "KE", "EB", "GT"))
                        IV, IVB = IVs[sl], IVBs[sl]
                        for tl in range(8):
                            ob = 6 + tl // 4
                            oc = (tl % 4) * P
                            k.op("pe", "matmul", reads=[IVB[tl // 4], As8B[tl]], writes=[psB[ob]], signal=False,
                                 out=ps[ob][:, oc:oc + P], lhsT=IV[:, tl, :], rhs=As8[:, tl, :], start=(tl % 4 == 0), stop=False,
                                 skip_group_check=True)
                            for c2 in range(2):
                                c = 2 * tl + c2
                                k.op("pe", "matmul", reads=[S16B[c], QDB], writes=[psB[ob]], signal=(c2 == 1),
                                     out=ps[ob][:, oc + c2 * 64:oc + (c2 + 1) * 64], lhsT=S16[:, c, :],
                                     rhs=QD[:, tl * P + c2 * 64:tl * P + (c2 + 1) * 64], start=False, stop=(c2 == 1),
                                     skip_group_check=True)
                        for hf in range(2):
                            k.op("act" if hf == 0 else "dve", "activation" if hf == 0 else "tensor_copy", reads=[psB[6 + hf]],
                                 writes=[Bf("REC")], out=A("REC")[:, hf * 512:(hf + 1) * 512], in_=ps[6 + hf][:],
                                 **({"func": AF.Copy} if hf == 0 else {}))
                        k.op("act", "activation", reads=[Bf("REC")], writes=[RSQB], out=RSQ[:], in_=A("REC")[:], func=AF.Square)
                        for cc in range(2):
                            cs = slice(cc * 512, (cc + 1) * 512)
                            k.op("pe", "matmul", reads=[RSQB, cB], writes=[psB[cc]], out=ps[cc][:], lhsT=ones[:], rhs=RSQ[:, cs],
                                 start=True, stop=True)
                            k.op("act", "activation", reads=[psB[cc]], writes=[Bf("LN1")], out=A("LN1")[:, cs], in_=ps[cc][:],
                                 func=AF.Ln, scale=1.0 / P, bias=EPS)
                        k.op("act", "activation", reads=[Bf("LN1")], writes=[Bf("RS")], out=A("RS")[:], in_=A("LN1")[:], func=AF.Exp,
                             scale=-0.5)
                        k.op("dve", "tensor_tensor", reads=[Bf("REC"), Bf("RS")], writes=[Bf("R1")], out=A("R1")[:], in0=A("REC")[:],
                             in1=A("RS")[:], op=ALU.mult)
                        m = mxc[0] % 2
                        mxc[0] += 1
                        k.op("dve", "tensor_tensor", reads=[Bf("R1"), GTB], writes=[MXB[m]], out=MX[m][:], in0=A("R1")[:],
                             in1=GT[:], op=ALU.mult)
                        k.dma("pool", mixed_d[512 + hh * P:512 + (hh + 1) * P, tk0:tk0 + SEG], MX[m][:], MXD[m], reads=[MXB[m]],
                              writes=[mixB[4 + hh]])

                    hwork = [(gi, groups[gi][1], sg) for gi in range(4, 8) for sg in range(T // SEG)]
                    for _ in stageA(*hwork[0], 0):
                        pass
                    for wi_ in range(len(hwork)):
                        stageB(*hwork[wi_], wi_ % 2)
                        if wi_ + 1 < len(hwork):
                            for _ in stageA(*hwork[wi_ + 1], (wi_ + 1) % 2):
                                pass
                        stageB2(*hwork[wi_], wi_ % 2)
            k.barrier()
            k.mark('C')
            if stop == 'C':
                k.mute = True
            with ExitStack() as fs:
                wup = sb("wup", [P, KC, 2 * DFF], BF16, fs)
                wdn = sb("wdn", [P, NFC, D], BF16, fs)
                wupB = [Buf() for _ in range(KC)]
                wdnB = [Buf() for _ in range(NFC)]
                cast_rr = [0]

                def cast(out, in_, scal, reads, writes):
                    e = ("pool", "act", "dve")[cast_rr[0] % 3]
                    cast_rr[0] += 1
                    if e == "pool":
                        if scal is None:
                            k.op("pool", "tensor_copy", reads=reads, writes=writes, out=out, in_=in_)
                        else:
                            k.op("pool", "tensor_scalar", reads=reads + [pvB], writes=writes, out=out, in0=in_, scalar1=scal,
                                 scalar2=1.0, op0=ALU.mult, op1=ALU.mult)
                    elif e == "act":
                        if scal is None:
                            k.op("act", "activation", reads=reads, writes=writes, out=out, in_=in_, func=AF.Copy)
                        else:
                            k.op("act", "activation", reads=reads + [pvB], writes=writes, out=out, in_=in_, func=AF.Copy,
                                 scale=scal)
                    else:
                        if scal is None:
                            k.op("dve", "tensor_copy", reads=reads, writes=writes, out=out, in_=in_)
                        else:
                            k.op("dve", "tensor_scalar", reads=reads + [pvB], writes=writes, out=out, in0=in_, scalar1=scal,
                                 scalar2=None, op0=ALU.mult)

                hB = [Buf("h%d" % i) for i in range(NT)]
                with ExitStack() as pd:
                    dsb = lambda n, s, d: sb(n, s, d, pd)
                    wo = dsb("wo", [P, KC, D], BF16)
                    woB = [Buf() for _ in range(KC)]
                    wstF = [dsb("wstF%d" % i, [P, 1408], F32) for i in range(2)]
                    wstFB = [Buf() for _ in range(2)]
                    wstFD = [k.dsem() for _ in range(2)]
                    wc = [0]

                    def stage(src_ap, ncols):
                        s = wc[0] % 2
                        wc[0] += 1
                        k.dma("sp", wstF[s][:, 0:ncols], src_ap, wstFD[s], writes=[wstFB[s]])
                        return s

                    for kc in range(KC):
                        s = stage(wout_d[kc * P:(kc + 1) * P, :], D)
                        cast(wo[:, kc, :], wstF[s][:, 0:D], pv[:, C_GMIX + kc:C_GMIX + kc + 1], [wstFB[s]], [woB[kc]])

                    def load_ffn_weights(step):
                        if step < 32:
                            kc, q = step // 4, step % 4
                            s = stage(wup_d[kc * P:(kc + 1) * P, q * 1408:(q + 1) * 1408], 1408)
                            cast(wup[:, kc, q * 1408:(q + 1) * 1408], wstF[s][:, 0:1408], pv[:, C_G2 + kc:C_G2 + kc + 1],
                                 [wstFB[s]], [wupB[kc]])

                    mxt = [dsb("mxt%d" % i, [P, KC, 512], BF16) for i in range(2)]
                    mxtB = [Buf() for _ in range(2)]
                    mxtD = [k.dsem() for _ in range(2)]
                    xin2 = [dsb("xin2_%d" % i, [P, D], F32) for i in range(2)]
                    xin2B = [Buf() for _ in range(2)]
                    xin2D = [k.dsem() for _ in range(2)]
                    hsb = [dsb("hsb%d" % i, [P, D], F32) for i in range(2)]
                    hsbB = [Buf() for _ in range(2)]
                    hsbD = [k.dsem() for _ in range(2)]
                    sq = [dsb("sq%d" % i, [P, 4, P], BF16) for i in range(2)]
                    sqB = [Buf() for _ in range(2)]
                    stD = dsb("stD", [P, 2 * NT], F32)
                    stDB = [Buf() for _ in range(NT)]
                    mixed_v = mixed_d.rearrange("(kc p) t -> p kc t", p=P)
                    wstep = 0
                    for i in range(NT):
                        c, tl = i // 4, i % 4
                        ms_ = c % 2
                        s = i % 2
                        if tl == 0:
                            k.dma("sp", mxt[ms_][:], mixed_v[:, :, c * 512:(c + 1) * 512], mxtD[ms_], reads=mixB,
                                  writes=[mxtB[ms_]])
                        k.dma("sp", xin2[s][:], x_d[i * P:(i + 1) * P, :], xin2D[s], writes=[xin2B[s]])
                        for _ in range(2):
                            load_ffn_weights(wstep)
                            wstep += 1
                        tsl = slice(tl * P, (tl + 1) * P)
                        k.op("pool", "tensor_tensor", reads=[mxtB[ms_]], writes=[sqB[s]], out=sq[s][:], in0=mxt[ms_][:, 0:4, tsl],
                             in1=mxt[ms_][:, 0:4, tsl], op=ALU.mult)
                        sbk = 4 + s
                        for kc in range(4):
                            k.op("pe", "matmul", reads=[sqB[s], cB], writes=[psB[sbk]], signal=(kc == 3), out=ps[sbk][:, 0:2],
                                 lhsT=sq[s][:, kc, :], rhs=ones[:, 0:2], start=(kc == 0), stop=(kc == 3))
                        rstd_chain(ps[sbk][:, 0:1], stD[:, i:i + 1], stD[:, NT + i:NT + i + 1], 512, [psB[sbk]], stDB[i])
                        for half in range(2):
                            ba, br = 2 * half, 2 * half + 1
                            hsl = slice(half * 512, (half + 1) * 512)
                            for kc in range(4):
                                k.op("pe", "matmul", reads=[mxtB[ms_], woB[kc]], writes=[psB[ba]], signal=(kc == 3), out=ps[ba][:],
                                     lhsT=mxt[ms_][:, kc, tsl], rhs=wo[:, kc, hsl], start=(kc == 0), stop=(kc == 3))
                            for kc in range(4, 8):
                                k.op("pe", "matmul", reads=[mxtB[ms_], woB[kc]], writes=[psB[br]], signal=(kc == 7), out=ps[br][:],
                                     lhsT=mxt[ms_][:, kc, tsl], rhs=wo[:, kc, hsl], start=(kc == 4), stop=(kc == 7))
                            k.op("dve", "tensor_tensor", reads=[psB[br], xin2B[s]], writes=[hsbB[s]], out=hsb[s][:, hsl], in0=ps[br][:],
                                 in1=xin2[s][:, hsl], op=ALU.add)
                            k.op("dve", "scalar_tensor_tensor", reads=[psB[ba], stDB[i]], writes=[hsbB[s]], out=hsb[s][:, hsl],
                                 in0=ps[ba][:], scalar=stD[:, NT + i:NT + i + 1], in1=hsb[s][:, hsl], op0=ALU.mult, op1=ALU.add)
                        k.dma("pool", h_d[i * P:(i + 1) * P, :], hsb[s][:], hsbD[s], reads=[hsbB[s]], writes=[hB[i]])
                    while wstep < 32:
                        load_ffn_weights(wstep)
                        wstep += 1
                    for g0 in range(0, NFC, 6):
                        wdnD = k.dsem()
                        wdnD.nobarrier = True
                        grp = list(range(g0, min(NFC, g0 + 6)))
                        for fc in grp:
                            ev = k.dma("pool", wdn[:, fc, :], wdn_d[fc * P:(fc + 1) * P, :], wdnD, writes=[Buf()])
                        for fc in grp:
                            wdnB[fc].w = ev
                k.barrier()
                k.mark('D')
                if stop == 'D':
                    k.mute = True
                with ExitStack() as pe_:
                    esb = lambda n, s, d: sb(n, s, d, pe_)
                    TT = 256
                    hs = [esb("hs%d" % i, [P, 2, D], F32) for i in range(3)]
                    hsB = [Buf() for _ in range(3)]
                    hsD = [k.dsem() for _ in range(3)]
                    u2b2 = [esb("u2b%d" % i, [P, D], BF16) for i in range(2)]
                    u2bB2 = [Buf() for _ in range(2)]
                    u2T = [esb("u2T%d" % i, [P, KC, TT + 2], BF16) for i in range(2)]
                    u2TB = [Buf() for _ in range(2)]
                    u2TQ = [[Buf() for _ in range(4)] for _ in range(2)]
                    hid = esb("hid", [P, NFC, TT], BF16)
                    hidB = [Buf() for _ in range(NFC)]
                    cbuf = [esb("cbuf%d" % i, [P, TT], F32) for i in range(3)]
                    cbufB = [Buf() for _ in range(3)]
                    ge = [esb("ge%d" % i, [P, TT], BF16) for i in range(3)]
                    geB = [Buf() for _ in range(3)]
                    vb = [esb("vb%d" % i, [P, TT], BF16) for i in range(3)]
                    vbB = [Buf() for _ in range(3)]
                    osb = [esb("osb%d" % i, [P, D], F32) for i in range(2)]
                    osbB = [Buf() for _ in range(2)]
                    osbD = [k.dsem() for _ in range(2)]
                    junkE = esb("junkE", [P, D], BF16)
                    junkEB = Buf()
                    stE = esb("stE", [P, 6 * NT], F32)
                    stEB = [Buf() for _ in range(2 * NT)]
                    h_v = h_d.rearrange("(j s p) d -> j p s d", s=2, p=P)
                    octr = 0
                    fcc = 0
                    def pro_dma(j):
                        s = j % 3
                        k.dma("sp", hs[s][:], h_v[j], hsD[s], reads=[hB[2 * j], hB[2 * j + 1]], writes=[hsB[s]])

                    def pro_chain(j):
                        s = j % 3
                        for sub in range(2):
                            ti = 2 * j + sub
                            k.op("act", "activation", reads=[hsB[s]], writes=[junkEB, stEB[ti]], out=junkE[:], in_=hs[s][:, sub, :],
                                 func=AF.Square, accum_out=stE[:, ti:ti + 1])
                            rstd_chain(stE[:, ti:ti + 1], stE[:, NT + ti:NT + ti + 1], stE[:, 2 * NT + ti:2 * NT + ti + 1], D,
                                       [stEB[ti]], stEB[ti])
                            u2b, u2bB = u2b2[sub], u2bB2[sub]
                            k.op("dve", "tensor_scalar", reads=[hsB[s], stEB[ti]], writes=[u2bB], out=u2b[:], in0=hs[s][:, sub, :],
                                 scalar1=stE[:, 2 * NT + ti:2 * NT + ti + 1], scalar2=None, op0=ALU.mult)

                    def pro_pe(j):
                        s = j % 2
                        if j == 0:
                            k.op("pool", "memset", writes=[u2TB[s]], ap=u2T[s][:, :, 0:2], constant=0.0)
                        else:
                            k.op("pool", "tensor_copy", reads=[u2TQ[1 - s][2], u2TQ[1 - s][3]], writes=[u2TB[s]], out=u2T[s][:, :, 0:2],
                                 in_=u2T[1 - s][:, :, TT:TT + 2])
                        for sub in range(2):
                            u2b, u2bB = u2b2[sub], u2bB2[sub]
                            b0_ = 6 if sub == 0 else 4
                            for kc in range(KC):
                                bkk = b0_ + kc // 4
                                k.op("pe", "matmul", reads=[u2bB, cB], writes=[psB[bkk]], signal=(kc % 4 == 3),
                                     out=ps[bkk][:, (kc % 4) * P:(kc % 4 + 1) * P], lhsT=u2b[:, kc * P:(kc + 1) * P],
                                     rhs=ident[:], start=True, stop=True, skip_group_check=True)
                            k.op("act", "activation", reads=[psB[b0_]], writes=[u2TQ[s][sub * 2]],
                                 out=u2T[s][:, 0:4, 2 + sub * P:2 + (sub + 1) * P],
                                 in_=ps[b0_][:].rearrange("p (c t) -> p c t", t=P), func=AF.Copy)
                            k.op("dve", "tensor_copy", reads=[psB[b0_ + 1]], writes=[u2TQ[s][sub * 2 + 1]],
                                 out=u2T[s][:, 4:8, 2 + sub * P:2 + (sub + 1) * P],
                                 in_=ps[b0_ + 1][:].rearrange("p (c t) -> p c t", t=P))

                    pro_dma(0)
                    for j in range(T // TT):
                        s = j % 2
                        if j == 0:
                            pro_dma(1)
                            pro_chain(0)
                            pro_pe(0)
                        if j + 2 < T // TT:
                            pro_dma(j + 2)
                        for fc in range(NFC):
                            pr = fcc % 3
                            fcc += 1
                            bg, bv_ = ((0, 1), (2, 3), (6, 7))[pr]
                            for kc in range(KC):
                                k.op("pe", "matmul", reads=[wupB[kc], u2TB[s]] + u2TQ[s], writes=[psB[bg]], signal=(kc == KC - 1),
                                     out=ps[bg][:, 0:TT + 2], lhsT=wup[:, kc, fc * P:(fc + 1) * P], rhs=u2T[s][:, kc, 0:TT + 2],
                                     start=(kc == 0), stop=(kc == KC - 1))
                            for kc in range(KC):
                                k.op("pe", "matmul", reads=[wupB[kc], u2TB[s]] + u2TQ[s], writes=[psB[bv_]], signal=(kc == KC - 1),
                                     out=ps[bv_][:, 0:TT], lhsT=wup[:, kc, DFF + fc * P:DFF + (fc + 1) * P],
                                     rhs=u2T[s][:, kc, 2:TT + 2], start=(kc == 0), stop=(kc == KC - 1))
                            cw = lambda jj: pv[:, C_CW + fc * 3 + jj:C_CW + fc * 3 + jj + 1]
                            k.op("dve", "tensor_scalar", reads=[psB[bg], pvB], writes=[cbufB[pr]], out=cbuf[pr][:],
                                 in0=ps[bg][:, 2:TT + 2], scalar1=cw(2), scalar2=pv[:, C_CB + fc:C_CB + fc + 1], op0=ALU.mult,
                                 op1=ALU.add)
                            k.op("dve", "scalar_tensor_tensor", reads=[psB[bg], pvB], writes=[cbufB[pr]], out=cbuf[pr][:],
                                 in0=ps[bg][:, 1:TT + 1], scalar=cw(1), in1=cbuf[pr][:], op0=ALU.mult, op1=ALU.add)
                            k.op("dve", "scalar_tensor_tensor", reads=[psB[bg], pvB], writes=[cbufB[pr]], out=cbuf[pr][:],
                                 in0=ps[bg][:, 0:TT], scalar=cw(0), in1=cbuf[pr][:], op0=ALU.mult, op1=ALU.add)
                            k.op("act", "activation", reads=[psB[bv_]], writes=[vbB[pr]], out=vb[pr][:], in_=ps[bv_][:, 0:TT], func=AF.Copy)
                            k.op("act", "activation", reads=[cbufB[pr]], writes=[geB[pr]], out=ge[pr][:], in_=cbuf[pr][:], func=AF.Gelu)
                            k.op("pool", "tensor_tensor", reads=[geB[pr], vbB[pr]], writes=[hidB[fc]], out=hid[:, fc, :], in0=vb[pr][:],
                                 in1=ge[pr][:], op=ALU.mult)
                        if j + 1 < T // TT:
                            pro_chain(j + 1)
                        epi = []
                        for sub in range(2):
                            ti = 2 * j + sub
                            o = octr % 2
                            octr += 1
                            for half in range(2):
                                bk = 4 + half
                                for fc in range(NFC):
                                    k.op("pe", "matmul", reads=[hidB[fc], wdnB[fc]], writes=[psB[bk]], signal=(fc == NFC - 1),
                                         out=ps[bk][:], lhsT=hid[:, fc, sub * P:(sub + 1) * P], rhs=wdn[:, fc, half * 512:(half + 1) * 512],
                                         start=(fc == 0), stop=(fc == NFC - 1))
                                k.op("dve", "tensor_tensor", reads=[psB[bk], hsB[j % 3]], writes=[osbB[o]],
                                     out=osb[o][:, half * 512:(half + 1) * 512], in0=ps[bk][:],
                                     in1=hs[j % 3][:, sub, half * 512:(half + 1) * 512], op=ALU.add)
                            k.op("act", "activation", reads=[osbB[o]], writes=[junkEB, stEB[NT + ti]], out=junkE[:], in_=osb[o][:],
                                 func=AF.Square, accum_out=stE[:, 3 * NT + ti:3 * NT + ti + 1])
                            rstd_chain(stE[:, 3 * NT + ti:3 * NT + ti + 1], stE[:, 4 * NT + ti:4 * NT + ti + 1],
                                       stE[:, 5 * NT + ti:5 * NT + ti + 1], D, [stEB[NT + ti]], stEB[NT + ti])
                            epi.append((ti, o))
                        if j + 1 < T // TT:
                            pro_pe(j + 1)
                        for ti, o in epi:
                            k.op("dve", "scalar_tensor_tensor", reads=[osbB[o], stEB[NT + ti], gFB], writes=[osbB[o]], out=osb[o][:],
                                 in0=osb[o][:], scalar=stE[:, 5 * NT + ti:5 * NT + ti + 1], in1=gF[:], op0=ALU.mult, op1=ALU.mult)
                            k.dma("pool", out_d[ti * P:(ti + 1) * P, :], osb[o][:], osbD[o], reads=[osbB[o]], is_out=True)
                    k.barrier()

        except Stop:
            pass
        k.mute = False
        k.barrier()

        with nc.Block() as block:
            @block.sync
            def _(h):
                k.replay("sp", h)

            @block.tensor
            def _(h):
                k.replay("pe", h)

            @block.scalar
            def _(h):
                k.replay("act", h)

            @block.vector
            def _(h):
                k.replay("dve", h)

            @block.gpsimd
            def _(h):
                k.replay("pool", h)
    build.marks = k.marks
    return nc


def _col(v):
    v = np.asarray(v, np.float32).reshape(-1, P)
    return np.ascontiguousarray(v.T)


def _prep(inputs):
    f = lambda a: np.ascontiguousarray(np.asarray(a, np.float32))
    pvec = np.zeros((P, 128), np.float32)
    pvec[:, 0:8] = _col(inputs["norm1_g"][0])
    pvec[:, 8:16] = _col(np.concatenate([np.asarray(inputs["attn_norm_g"][0]), np.asarray(inputs["hgrn_norm_g"][0])]))
    pvec[:, 16:24] = _col(inputs["norm2_g"][0])
    lbl = np.asarray(inputs["hgrn_lb_logits"], np.float32)
    pvec[:, 24:28] = _col(lbl[0])
    pvec[:, 28:32] = _col(lbl[1])
    cw = np.asarray(inputs["conv_w"][0], np.float32)
    for j in range(3):
        pvec[:, 32 + j:32 + 66:3] = _col(cw[j])
    pvec[:, 98:120] = _col(inputs["conv_b"][0])
    shared = {"w_in": f(inputs["w_in"][0]), "w_out": f(inputs["w_out"][0]), "w_up": f(inputs["w_up"][0]),
              "w_down": f(inputs["w_down"][0]), "pvec": pvec, "gfin": f(inputs["final_norm_g"])}
    x = f(inputs["x"])
    return [dict(shared, x=x[i]) for i in range(NCORES)]


def kernel(**inputs):
    nc = build()
    in_maps = _prep(inputs)
    res = run_bass_kernel_spmd(nc, in_maps, core_ids=list(range(NCORES)))
    return np.stack([np.asarray(r["out"], np.float32) for r in res.results], axis=0)
```
